# Optimizing a Trainium2 kernel written in Bass

```python
import jax, jax.numpy as jnp
from jax import lax
import numpy as np

D_MODEL = 1024
BATCH = 8
SEQ = 8192
DEPTH = 1

CHUNK = 64
Q_BLOCK = 128
D_FF = 2816
HG_HEADS = 8
HG_HEAD_K = 128
HG_HEAD_V = 128
HG_WIDTH = HG_HEADS * HG_HEAD_K
HG_VWIDTH = HG_HEADS * HG_HEAD_V
MLA_HEADS = 8
MLA_NOPE = 128
MLA_ROPE = 64
MLA_V = 128
MLA_QK = MLA_NOPE + MLA_ROPE
Q_LORA = 384
KV_LORA = 256
ROPE_THETA = 10000.0
EPS = 1e-6
IN_SPLITS = (HG_WIDTH, HG_WIDTH, HG_VWIDTH, HG_VWIDTH, Q_LORA, KV_LORA, MLA_ROPE)
IN_COLS = 2 * HG_WIDTH + 2 * HG_VWIDTH + Q_LORA + KV_LORA + MLA_ROPE

kernel_name = 'hybrid_hgrn2_mla_macaron'


def _rms_norm(x, gain):
    xf = x.astype(jnp.float32)
    y = xf * lax.rsqrt(jnp.mean(xf * xf, axis=-1, keepdims=True) + EPS)
    return (y * gain.astype(jnp.float32)).astype(x.dtype)


def _swiglu(x, w_in, w_out):
    gate, up = jnp.split(x @ w_in, 2, axis=-1)
    return (jax.nn.silu(gate) * up) @ w_out


def _rotate(x, cos, sin):
    half = x.shape[-1] // 2
    x1, x2 = x[..., :half], x[..., half:]
    return jnp.concatenate([x1 * cos - x2 * sin, x2 * cos + x1 * sin], axis=-1)


def _to_chunks(t):
    b, s, h, d = t.shape
    return t.reshape(b, s // CHUNK, CHUNK, h, d).transpose(1, 0, 3, 2, 4)


def _from_chunks(t):
    n, b, h, c, d = t.shape
    return t.transpose(1, 0, 3, 2, 4).reshape(b, n * c, h, d)


def _hgrn2_chunk_step(state, inputs):
    q, k, v, log_f = inputs
    cum = jnp.cumsum(log_f, axis=2)
    o_inter = jnp.einsum('bhtk,bhkv->bhtv', q * jnp.exp(cum), state)
    causal = jnp.tril(jnp.ones((CHUNK, CHUNK), dtype=bool))[:, :, None]
    rel = cum[:, :, :, None, :] - cum[:, :, None, :, :]
    decay = jnp.exp(jnp.where(causal, rel, -jnp.inf))
    scores = jnp.einsum('bhtk,bhtsk,bhsk->bhts', q, decay, k)
    o_intra = jnp.einsum('bhts,bhsv->bhtv', scores, v)
    last = cum[:, :, -1, :]
    new_state = jnp.exp(last)[..., None] * state + jnp.einsum(
        'bhsk,bhsv->bhkv', k * jnp.exp(last[:, :, None, :] - cum), v)
    return new_state, o_intra + o_inter


def _hgrn2(q_raw, f_raw, i_raw, g_raw, lower_bound, out_gain):
    b, s, _ = q_raw.shape
    f32 = jnp.float32
    q = jax.nn.silu(q_raw.astype(f32)).reshape(b, s, HG_HEADS, HG_HEAD_K)
    z = f_raw.astype(f32).reshape(b, s, HG_HEADS, HG_HEAD_K)
    lb = lower_bound.astype(f32).reshape(HG_HEADS, HG_HEAD_K)
    log_f = jnp.logaddexp(jnp.log(lb), jnp.log1p(-lb) + jax.nn.log_sigmoid(z))
    k = -jnp.expm1(log_f)
    v = i_raw.astype(f32).reshape(b, s, HG_HEADS, HG_HEAD_V)
    state0 = jnp.zeros((b, HG_HEADS, HG_HEAD_K, HG_HEAD_V), f32)
    _, o = lax.scan(_hgrn2_chunk_step, state0,
                    (_to_chunks(q), _to_chunks(k), _to_chunks(v), _to_chunks(log_f)))
    o = _rms_norm(_from_chunks(o), out_gain)
    o = o * jax.nn.silu(g_raw.astype(f32)).reshape(b, s, HG_HEADS, HG_HEAD_V)
    return o.reshape(b, s, HG_VWIDTH).astype(q_raw.dtype)


def _mla(c_q, c_kv, k_pe, positions, q_lora_gain, w_q_up, kv_lora_gain, w_kv_up,
         q_head_gain, k_head_gain):
    b, s, _ = c_q.shape
    q = (_rms_norm(c_q, q_lora_gain) @ w_q_up).reshape(b, s, MLA_HEADS, MLA_QK)
    kv = (_rms_norm(c_kv, kv_lora_gain) @ w_kv_up).reshape(b, s, MLA_HEADS, MLA_NOPE + MLA_V)
    k_nope, v = kv[..., :MLA_NOPE], kv[..., MLA_NOPE:]
    k = jnp.concatenate(
        [k_nope, jnp.broadcast_to(k_pe[:, :, None, :], (b, s, MLA_HEADS, MLA_ROPE))], axis=-1)
    q = _rms_norm(q, q_head_gain)
    k = _rms_norm(k, k_head_gain)
    inv_freq = ROPE_THETA ** (-jnp.arange(0, MLA_ROPE, 2, dtype=jnp.float32) / MLA_ROPE)
    ang = positions.astype(jnp.float32)[:, :, None, None] * inv_freq
    cos = jnp.cos(ang).astype(q.dtype)
    sin = jnp.sin(ang).astype(q.dtype)
    q = jnp.concatenate([q[..., :MLA_NOPE], _rotate(q[..., MLA_NOPE:], cos, sin)], axis=-1)
    k = jnp.concatenate([k[..., :MLA_NOPE], _rotate(k[..., MLA_NOPE:], cos, sin)], axis=-1)
    n_blocks = s // Q_BLOCK
    q_blocks = q.reshape(b, n_blocks, Q_BLOCK, MLA_HEADS, MLA_QK).transpose(1, 0, 2, 3, 4)
    key_chunk = jnp.arange(s) // CHUNK
    scale = MLA_QK ** -0.5

    def attend(args):
        q_blk, blk = args
        q_chunk = (blk * Q_BLOCK + jnp.arange(Q_BLOCK)) // CHUNK
        allowed = key_chunk[None, :] <= q_chunk[:, None]
        logits = jnp.einsum('bqhd,bkhd->bhqk', q_blk, k).astype(jnp.float32) * scale
        logits = jnp.where(allowed, logits, -jnp.inf)
        probs = jax.nn.softmax(logits, axis=-1).astype(v.dtype)
        return jnp.einsum('bhqk,bkhd->bqhd', probs, v)

    o = lax.map(attend, (q_blocks, jnp.arange(n_blocks)))
    return o.transpose(1, 0, 2, 3, 4).reshape(b, s, MLA_HEADS * MLA_V)


def setup_inputs(seed: int = 0) -> dict:
    key = jax.random.key(seed)
    ks = jax.random.split(key, 24)
    f32 = jnp.float32
    L = DEPTH

    def w(k, shape, fan_in):
        return jax.random.normal(k, shape, f32) * (fan_in ** -0.5)

    def gain(k, shape):
        return 1.0 + 0.05 * jax.random.normal(k, shape, f32)

    x = jax.random.normal(ks[0], (BATCH, SEQ, D_MODEL), f32)
    offsets = jax.random.randint(ks[1], (BATCH, 1), 0, 64, dtype=jnp.int32) * CHUNK
    positions = (offsets + jnp.arange(SEQ, dtype=jnp.int32)[None, :]).astype(jnp.int32)
    return {
        'x': x,
        'positions': positions,
        'ffn1_norm': gain(ks[2], (L, D_MODEL)),
        'ffn1_w_in': w(ks[3], (L, D_MODEL, 2 * D_FF), D_MODEL),
        'ffn1_w_out': w(ks[4], (L, D_FF, D_MODEL), D_FF),
        'mix_norm': gain(ks[5], (L, D_MODEL)),
        'w_in': w(ks[6], (L, D_MODEL, IN_COLS), D_MODEL),
        'hg_lb_table': 0.5 * jax.random.normal(ks[7], (L + 1, HG_WIDTH), f32),
        'hg_out_norm': gain(ks[8], (L, HG_HEAD_V)),
        'w_hg_branch': w(ks[9], (L, HG_VWIDTH, D_MODEL), HG_VWIDTH),
        'mla_q_lora_norm': gain(ks[10], (L, Q_LORA)),
        'w_q_up': w(ks[11], (L, Q_LORA, MLA_HEADS * MLA_QK), Q_LORA),
        'mla_kv_lora_norm': gain(ks[12], (L, KV_LORA)),
        'w_kv_up': w(ks[13], (L, KV_LORA, MLA_HEADS * (MLA_NOPE + MLA_V)), KV_LORA),
        'q_head_norm': gain(ks[14], (L, MLA_QK)),
        'k_head_norm': gain(ks[15], (L, MLA_QK)),
        'w_mla_branch': w(ks[16], (L, MLA_HEADS * MLA_V, D_MODEL), MLA_HEADS * MLA_V),
        'w_merge': w(ks[17], (L, D_MODEL, 2 * D_MODEL), D_MODEL),
        'b_merge': 0.02 * jax.random.normal(ks[18], (L, 2 * D_MODEL), f32),
        'w_out': w(ks[19], (L, D_MODEL, D_MODEL), D_MODEL),
        'ffn2_norm': gain(ks[20], (L, D_MODEL)),
        'ffn2_w_in': w(ks[21], (L, D_MODEL, 2 * D_FF), D_MODEL),
        'ffn2_w_out': w(ks[22], (L, D_FF, D_MODEL), D_FF),
        'final_norm': gain(ks[23], (L, D_MODEL)),
    }


def reference(x, positions, ffn1_norm, ffn1_w_in, ffn1_w_out, mix_norm, w_in, hg_lb_table,
              hg_out_norm, w_hg_branch, mla_q_lora_norm, w_q_up, mla_kv_lora_norm, w_kv_up,
              q_head_norm, k_head_norm, w_mla_branch, w_merge, b_merge, w_out,
              ffn2_norm, ffn2_w_in, ffn2_w_out, final_norm):
    lower_bounds = jnp.cumsum(jax.nn.softmax(hg_lb_table.astype(jnp.float32), axis=0), axis=0)
    split_at = np.cumsum(IN_SPLITS)[:-1].tolist()
    h = x
    for l in range(DEPTH):
        h = h + 0.5 * _swiglu(_rms_norm(h, ffn1_norm[l]), ffn1_w_in[l], ffn1_w_out[l])
        u = _rms_norm(h, mix_norm[l])
        hg_q, hg_f, hg_i, hg_g, c_q, c_kv, k_pe = jnp.split(u @ w_in[l], split_at, axis=-1)
        y_hg = _hgrn2(hg_q, hg_f, hg_i, hg_g, lower_bounds[l], hg_out_norm[l]) @ w_hg_branch[l]
        y_mla = _mla(c_q, c_kv, k_pe, positions, mla_q_lora_norm[l], w_q_up[l],
                     mla_kv_lora_norm[l], w_kv_up[l], q_head_norm[l], k_head_norm[l]) @ w_mla_branch[l]
        g_hg, g_mla = jnp.split(jax.nn.sigmoid(u @ w_merge[l] + b_merge[l]), 2, axis=-1)
        h = h + (g_hg * y_hg + g_mla * y_mla) @ w_out[l]
        h = h + 0.5 * _swiglu(_rms_norm(h, ffn2_norm[l]), ffn2_w_in[l], ffn2_w_out[l])
        h = _rms_norm(h, final_norm[l])
    return h
```

```python
import math
from contextlib import ExitStack

import numpy as np
import ml_dtypes
import concourse.bass as bass
import concourse.mybir as mybir
from concourse.bass_utils import run_bass_kernel_spmd

F32 = mybir.dt.float32
BF16 = mybir.dt.bfloat16
I32 = mybir.dt.int32
AF = mybir.ActivationFunctionType
ALU = mybir.AluOpType

D = 1024
DFF = 2816
T = 512
NH = 8
EPS = 1e-6
SCALE = 192.0 ** -0.5
MAGIC = 12582912.0
TWO_PI = 6.28318
ENGS = ["pe", "act", "dve", "pool", "sp"]
SEM_CAP = 2000
DSEM_CAP = 100
SAME_WIN = 3

C_FFN1, C_MIX, C_FFN2, C_FIN = 0, 8, 16, 24
C_QL, C_KVL, C_HGO = 32, 35, 37
C_GQN, C_GQR, C_GKN, C_GKR = 38, 39, 40, 41
C_BM = 42
C_LB0, C_LB1 = 58, 66
C_INVF, C_S2PI = 74, 75
NCOL = 76


class Buf:
    __slots__ = ("name", "lw", "rd", "rdd", "excl")

    def __init__(self, name, excl=False):
        self.name = name
        self.excl = excl
        self.lw = None
        self.rd = {}
        self.rdd = []


class Op:
    __slots__ = ("eng", "fn", "deps", "idx", "pub", "ticket", "dkey", "dval", "waits")


class Tracker:
    def __init__(self, nc, tag):
        self.nc = nc
        self.tag = tag
        self.ops = {e: [] for e in ENGS}
        self.dcount = {}

    def op(self, eng, fn, rd=(), wr=(), dkey=None):
        o = Op()
        o.eng, o.fn, o.dkey = eng, fn, dkey
        o.pub, o.ticket, o.waits = False, 0, None
        o.idx = len(self.ops[eng])
        if dkey is not None:
            self.dcount[dkey] = self.dcount.get(dkey, 0) + 1
            o.dval = self.dcount[dkey]
        else:
            o.dval = 0
        deps = {}
        ex = [b for b in rd if b.excl and b not in wr]
        if ex:
            wr = list(wr) + ex

        def add(d):
            if d is None or d is o:
                return
            if d.dkey is not None:
                deps[("d", d.dkey, d.dval)] = d
            else:
                k = ("e", d.eng)
                c = deps.get(k)
                if c is None or c.idx < d.idx:
                    deps[k] = d

        for b in rd:
            add(b.lw)
        for b in wr:
            add(b.lw)
            for r in b.rd.values():
                add(r)
            for r in b.rdd:
                add(r)
        o.deps = list(deps.values())
        for b in rd:
            if dkey is not None:
                b.rdd.append(o)
            else:
                b.rd[eng] = o
        for b in wr:
            b.lw = o
            b.rd = {}
            b.rdd = []
        self.ops[eng].append(o)
        return o

    def finalize(self):
        for e in ENGS:
            seen = {}
            for o in self.ops[e]:
                need = {}
                for d in o.deps:
                    if d.dkey is not None:
                        k = ("d", d.dkey)
                        if seen.get(k, 0) < d.dval:
                            c = need.get(k)
                            if c is None or c.dval < d.dval:
                                need[k] = d
                    else:
                        if d.eng == e:
                            if e == "pe" or e == "sp":
                                continue
                            if o.idx - d.idx > SAME_WIN:
                                continue
                        k = ("e", d.eng)
                        if seen.get(k, -1) < d.idx:
                            c = need.get(k)
                            if c is None or c.idx < d.idx:
                                need[k] = d
                for k, d in need.items():
                    if k[0] == "d":
                        seen[k] = d.dval
                    else:
                        seen[k] = d.idx
                        d.pub = True
                o.waits = list(need.values())
        self.nsem = {}
        for e in ENGS:
            t = 0
            for o in self.ops[e]:
                if o.dkey is None and o.pub:
                    t += 1
                    o.ticket = t
            self.nsem[e] = (t + SEM_CAP - 1) // SEM_CAP

    def alloc_sems(self, stack):
        nc = self.nc
        self.esems = {}
        for e in ENGS:
            self.esems[e] = [stack.enter_context(nc.semaphore(f"{self.tag}_{e}{k}"))
                             for k in range(self.nsem[e])]
        self.dsems = {}
        for key, n in self.dcount.items():
            ng = (n + DSEM_CAP - 1) // DSEM_CAP
            self.dsems[key] = [stack.enter_context(nc.semaphore(f"{self.tag}_d{key}_{g}"))
                               for g in range(ng)]

    def _esem(self, e, ticket):
        return self.esems[e][(ticket - 1) // SEM_CAP], (ticket - 1) % SEM_CAP + 1

    def _dsem(self, key, n):
        return self.dsems[key][(n - 1) // DSEM_CAP], 16 * ((n - 1) % DSEM_CAP + 1)

    def emit_engine(self, e, eng):
        for o in self.ops[e]:
            for d in o.waits:
                if d.dkey is not None:
                    s, v = self._dsem(d.dkey, d.dval)
                else:
                    s, v = self._esem(d.eng, d.ticket)
                eng.wait_ge(s, v)
            if o.fn is None:
                continue
            inst = o.fn(eng)
            if o.dkey is not None:
                s, v = self._dsem(o.dkey, o.dval)
                inst.then_inc(s, 16)
            elif o.pub:
                s, v = self._esem(e, o.ticket)
                inst.then_inc(s, 1)

    def run_block(self):
        nc = self.nc
        with nc.Block() as block:
            @block.sync
            def _(eng):
                self.emit_engine("sp", eng)

            @block.tensor
            def _(eng):
                self.emit_engine("pe", eng)

            @block.scalar
            def _(eng):
                self.emit_engine("act", eng)

            @block.vector
            def _(eng):
                self.emit_engine("dve", eng)

            @block.gpsimd
            def _(eng):
                self.emit_engine("pool", eng)


class Ring:
    def __init__(self, items):
        self.items = items
        self.i = 0

    def get(self):
        it = self.items[self.i % len(self.items)]
        self.i += 1
        return it


PIECE_ELEMS = 4096


def piece_specs():
    P = []

    def ffn(pfx, gcol):
        for j2 in range(11):
            P.append((f"{pfx}_wi{j2}", 4096, [
                (f"{pfx}_w_in", 0, 8, 256 * j2, 256, 0, 256, gcol, 1),
                (f"{pfx}_w_in", 0, 8, DFF + 256 * j2, 256, 2048, 256, gcol, 1)]))
        for m in range(8):
            P.append((f"{pfx}_wo{m}", 2816, [
                (f"{pfx}_w_out", 0, 22, 128 * m, 128, 0, 128, None, 0)]))

    ffn("ffn1", C_FFN1)
    for g in range(2):
        for nm, base in (("hgi", 2048), ("hgf", 1024), ("hgq", 0), ("hgg", 3072)):
            P.append((f"{nm}{g}", 4096, [("w_in", 0, 8, base + 512 * g, 512, 0, 512, C_MIX, 1)]))
    P.append(("cq", 3584, [("w_in", 0, 8, 4096, 384, 0, 384, C_MIX, 1),
                           ("w_in", 0, 8, 4736, 64, 3072, 64, C_MIX, 1)]))
    P.append(("ckv", 2048, [("w_in", 0, 8, 4480, 256, 0, 256, C_MIX, 1)]))
    for g in range(2):
        P.append((f"qup{g}", 2304, [("w_q_up", 0, 3, 768 * g, 768, 0, 768, C_QL, 1)]))
    P.append(("kvk", 2048, [("w_kv_up", 0, 2, 256 * h, 128, 256 * h, 128, C_KVL, 1) for h in range(8)]))
    P.append(("kvv", 2048, [("w_kv_up", 0, 2, 256 * h + 128, 128, 128 * h, 1024, C_KVL, 1) for h in range(8)]))
    for m in range(8):
        P.append((f"mix{m}", 4096, [
            ("w_hg_branch", 0, 8, 128 * m, 128, 0, 128, C_HGO, 0),
            ("w_merge", 0, 8, 128 * m, 128, 1024, 128, C_MIX, 1),
            ("w_mla_branch", 0, 8, 128 * m, 128, 2048, 128, None, 0),
            ("w_merge", 0, 8, 1024 + 128 * m, 128, 3072, 128, C_MIX, 1)]))
    for g in range(2):
        P.append((f"wout{g}", 4096, [("w_out", 0, 8, 512 * g, 512, 0, 512, None, 0)]))
    ffn("ffn2", C_FFN2)
    return P


WSHAPES = {
    "ffn1_w_in": (D, 2 * DFF), "ffn1_w_out": (DFF, D), "w_in": (D, 4800),
    "w_hg_branch": (D, D), "w_q_up": (384, 1536), "w_kv_up": (256, 2048),
    "w_mla_branch": (D, D), "w_merge": (D, 2 * D), "w_out": (D, D),
    "ffn2_w_in": (D, 2 * DFF), "ffn2_w_out": (DFF, D),
}


def build(NT, debug=False, stages=9):
    S = NT * T
    nc = bass.Bass("TRN2", target_bir_lowering=False)
    xT_d = nc.dram_tensor("xT", [D, S], F32, kind="ExternalInput").ap()
    pos_d = nc.dram_tensor("pos", [1, S], I32, kind="ExternalInput").ap()
    consts_d = nc.dram_tensor("consts", [128, NCOL], F32, kind="ExternalInput").ap()
    mask_d = nc.dram_tensor("mask128", [128, 128], F32, kind="ExternalInput").ap()
    reset_d = nc.dram_tensor("resetm", [128, T], F32, kind="ExternalInput").ap()
    ident_d = nc.dram_tensor("ident", [128, 128], BF16, kind="ExternalInput").ap()
    ones_d = nc.dram_tensor("onesb", [128, 128], BF16, kind="ExternalInput").ap()
    W_d = {n: nc.dram_tensor(n, list(s), F32, kind="ExternalInput").ap() for n, s in WSHAPES.items()}
    outT_d = nc.dram_tensor("outT", [D, S], F32, kind="ExternalOutput").ap()
    specs = piece_specs()
    NP = len(specs)
    wp_d = nc.dram_tensor("wpieces", [NP, 128, PIECE_ELEMS], BF16, kind="Internal").ap()
    kvc_d = nc.dram_tensor("kvcache", [NT, 128, NH * 1536], BF16, kind="Internal").ap()
    dbg_d = {}

    with ExitStack() as semstack:
        tp = Tracker(nc, "p")
        with ExitStack() as st:
            cst = st.enter_context(nc.sbuf_tensor("p_consts", [128, NCOL], F32))
            NSTG = 2
            stg_f = [st.enter_context(nc.sbuf_tensor(f"p_sf{i}", [128, PIECE_ELEMS], F32)) for i in range(NSTG)]
            stg_b = [st.enter_context(nc.sbuf_tensor(f"p_sb{i}", [128, PIECE_ELEMS], BF16)) for i in range(NSTG)]
            b_cst = Buf("cst")
            b_sf = [Buf(f"sf{i}") for i in range(NSTG)]
            b_sb = [Buf(f"sb{i}") for i in range(NSTG)]
            b_wp = [Buf(f"wp{i}") for i in range(NP)]
            tp.op("sp", lambda e: e.dma_start(out=cst[:], in_=consts_d[:, :]), wr=[b_cst], dkey="c")
            for pi, (name, size, segs) in enumerate(specs):
                sl = pi % NSTG
                sf, sb = stg_f[sl], stg_b[sl]
                for (wn, r0, nk, c0, ncols, doff, kst, gcol, gstep) in segs:
                    src = W_d[wn][r0:r0 + nk * 128, c0:c0 + ncols].rearrange("(k p) c -> p k c", p=128)
                    dst = sf[:, doff:doff + nk * kst].rearrange("p (k c) -> p k c", c=kst)[:, :, 0:ncols]
                    tp.op("sp", (lambda e, dst=dst, src=src: e.dma_start(out=dst, in_=src)),
                          wr=[b_sf[sl]], dkey=f"l{sl}")
                use_act = (pi % 2 == 1)
                eng = "act" if use_act else "dve"
                anyg = any(s[7] is not None for s in segs)
                if not anyg:
                    if use_act:
                        fn = (lambda e, sb=sb, sf=sf, size=size:
                              e.activation(out=sb[:, 0:size], in_=sf[:, 0:size], func=AF.Copy))
                    else:
                        fn = (lambda e, sb=sb, sf=sf, size=size:
                              e.tensor_copy(out=sb[:, 0:size], in_=sf[:, 0:size]))
                    tp.op(eng, fn, rd=[b_sf[sl]], wr=[b_sb[sl]])
                else:
                    for (wn, r0, nk, c0, ncols, doff, kst, gcol, gstep) in segs:
                        for k in range(nk):
                            o_ = sb[:, doff + k * kst: doff + k * kst + ncols]
                            i_ = sf[:, doff + k * kst: doff + k * kst + ncols]
                            if gcol is None:
                                if use_act:
                                    fn = (lambda e, o_=o_, i_=i_: e.activation(out=o_, in_=i_, func=AF.Copy))
                                else:
                                    fn = (lambda e, o_=o_, i_=i_: e.tensor_copy(out=o_, in_=i_))
                            else:
                                g_ = cst[:, gcol + k * gstep: gcol + k * gstep + 1]
                                if use_act:
                                    fn = (lambda e, o_=o_, i_=i_, g_=g_:
                                          e.activation(out=o_, in_=i_, func=AF.Copy, scale=g_))
                                else:
                                    fn = (lambda e, o_=o_, i_=i_, g_=g_:
                                          e.tensor_scalar(out=o_, in0=i_, scalar1=g_, scalar2=None, op0=ALU.mult))
                            tp.op(eng, fn, rd=[b_sf[sl], b_cst], wr=[b_sb[sl]])
                tp.op("pool", (lambda e, pi=pi, sb=sb, size=size:
                               e.dma_start(out=wp_d[pi, :, 0:size], in_=sb[:, 0:size])),
                      rd=[b_sb[sl]], wr=[b_wp[pi]], dkey=f"s{sl}")
            tp.op("sp", None, rd=b_wp)
            tp.op("pool", None, rd=b_wp)
            tp.finalize()
            tp.alloc_sems(semstack)
            tp.run_block()

        tk = Tracker(nc, "m")
        with ExitStack() as st:
            def sb(name, shape, dt):
                return st.enter_context(nc.sbuf_tensor(name, shape, dt))

            CST = sb("CST", [128, NCOL], F32)
            MISC = sb("MISC", [128, 32], F32)
            MASK = sb("MASK", [128, 128], F32)
            RESET = sb("RESET", [128, T], F32)
            IDENT = sb("IDENT", [128, 128], BF16)
            ONES = sb("ONES", [128, 128], BF16)
            XT = sb("XT", [128, 8, T], F32)
            XN = sb("XN", [128, 8, T], BF16)
            AR1 = sb("AR1", [128, 24 * 512], BF16)
            HID = AR1[:, 0:22 * 512].rearrange("p (c t) -> p c t", t=512)
            CU = AR1[:, :].rearrange("p (h e) -> p h e", e=1536)
            NW = 3
            WR = [sb(f"WR{i}", [128, PIECE_ELEMS], BF16) for i in range(NW)]
            NTMP = 8
            TMP = [sb(f"TMP{i}", [128, T], F32) for i in range(NTMP)]
            NSQ = 4
            SQ = [sb(f"SQ{i}", [128, T], BF16) for i in range(NSQ)]
            QT = sb("QT", [128, 4, T], BF16)
            KT = sb("KT", [128, 4, T], BF16)
            EC4 = sb("EC4", [128, 4, T], F32)
            VTOK = sb("VTOK", [128, 4, 512], BF16)
            G = sb("G", [128, 4, T], BF16)
            SS_ = sb("S", [128, NH, 128], F32)
            SB16 = sb("SB16", [128, NH, 128], BF16)
            EL = sb("EL", [128, NH, 8], F32)
            KTOK = [sb(f"KTOK{i}", [128, 128], BF16) for i in range(4)]
            ATM = [sb(f"ATM{i}", [128, 128], BF16) for i in range(4)]
            YH = sb("YH", [128, 8, T], BF16)
            ATT = sb("ATT", [128, 8, T], BF16)
            QM = sb("QM", [128, 8, T], BF16)
            QRT = sb("QRT", [128, 8, T], BF16)
            CQN = sb("CQN", [128, 3, T], BF16)
            CKVN = sb("CKVN", [128, 2, T], BF16)
            KPR = sb("KPR", [128, T], F32)
            SQPE = sb("SQPE", [128, T], BF16)
            CC = sb("CC", [128, T], F32)
            NS = sb("NS", [128, T], F32)
            POSI = sb("POSI", [128, T], I32)
            NC_ = 3
            CR = [sb(f"CR{i}", [128, 1536], BF16) for i in range(NC_)]
            NPT = 4
            PT = [sb(f"PT{i}", [128, T], BF16) for i in range(NPT)]
            PS = [st.enter_context(nc.psum_tensor(f"PS{i}", [128, 512], F32)) for i in range(6)]
            PSQ = st.enter_context(nc.psum_tensor("PSQ", [128, 512], F32))
            PSB = st.enter_context(nc.psum_tensor("PSB", [128, 1024], BF16))

            b_CST, b_MISC, b_MASK, b_RESET, b_ID, b_ONES = (Buf(n) for n in
                                                            ("CST", "MISC", "MASK", "RESET", "ID", "ONES"))
            b_XT = [Buf(f"XT{c}") for c in range(8)]
            b_XN = [Buf(f"XN{c}") for c in range(8)]
            b_AR = [Buf(f"AR{c}") for c in range(24)]
            b_HID = b_AR[:22]

            def b_CU(h):
                return b_AR[3 * h:3 * h + 3]

            wring = Ring([(WR[i], Buf(f"WR{i}")) for i in range(NW)])
            tmpr = Ring([(TMP[i], Buf(f"TMP{i}")) for i in range(NTMP)])
            sqr = Ring([(SQ[i], Buf(f"SQ{i}")) for i in range(NSQ)])
            psr = Ring([(PS[i], Buf(f"PS{i}", excl=True)) for i in range(6)])
            b_PS = [psr.items[i][1] for i in range(6)]
            psq = Ring([(PS[4][:, 0:128], b_PS[4]), (PS[5][:, 0:128], b_PS[5]),
                        (PSQ[:, 0:128], Buf("PSQ", excl=True))])
            psb = Ring([(PSB[:, 0:128], Buf("PSB", excl=True))])
            ktokr = Ring([(KTOK[i], Buf(f"KTOK{i}")) for i in range(4)])
            atmr = Ring([(ATM[i], Buf(f"ATM{i}")) for i in range(4)])
            cring = Ring([(CR[i], Buf(f"CR{i}")) for i in range(NC_)])
            ptr = Ring([(PT[i], Buf(f"PT{i}")) for i in range(NPT)])
            b_QT = [Buf(f"QT{h}") for h in range(4)]
            b_KT = [Buf(f"KT{h}") for h in range(4)]
            b_EC = [Buf(f"EC{h}") for h in range(4)]
            b_VT = [Buf(f"VT{b}") for b in range(4)]
            b_G = [Buf(f"G{h}") for h in range(4)]
            b_S = [Buf(f"S{h}") for h in range(NH)]
            b_S16 = [Buf(f"S16{h}") for h in range(NH)]
            b_EL = [Buf(f"EL{h}") for h in range(NH)]
            b_YH = [Buf(f"YH{h}") for h in range(8)]
            b_ATT = [Buf(f"ATT{h}") for h in range(8)]
            b_QM = [Buf(f"QM{h}") for h in range(8)]
            b_QRT = [Buf(f"QRT{h}") for h in range(8)]
            b_CQN = [Buf(f"CQN{c}") for c in range(3)]
            b_CKVN = [Buf(f"CKVN{c}") for c in range(2)]
            b_KPR, b_SQPE, b_CC, b_NS, b_POSI = (Buf(n) for n in ("KPR", "SQPE", "CC", "NS", "POSI"))
            b_KVC = [Buf(f"KVC{i}") for i in range(NT)]
            b_OUT = [Buf(f"OUT{i}") for i in range(NT)]

            def mm(out, pairs, rd, wr):
                def fn(e):
                    n = len(pairs)
                    inst = None
                    for q, (l, r) in enumerate(pairs):
                        inst = e.matmul(out, lhsT=l, rhs=r, start=(q == 0), stop=(q == n - 1))
                    return inst
                return tk.op("pe", fn, rd=rd, wr=wr)

            def mm1(out, l, r, start, stop, rd, wr):
                return tk.op("pe", lambda e: e.matmul(out, lhsT=l, rhs=r, start=start, stop=stop),
                             rd=rd + wr if not start else rd, wr=wr)

            def act(out, in_, func, rd, wr, scale=None, bias=None):
                kw = {}
                if scale is not None:
                    kw["scale"] = scale
                if bias is not None:
                    kw["bias"] = bias
                return tk.op("act", lambda e: e.activation(out=out, in_=in_, func=func, **kw), rd=rd, wr=wr)

            def tt(out, in0, in1, op, rd, wr):
                return tk.op("dve", lambda e: e.tensor_tensor(out=out, in0=in0, in1=in1, op=op), rd=rd, wr=wr)

            def ts(out, in0, s1, op0, rd, wr, s2=None, op1=None):
                if op1 is None:
                    return tk.op("dve", lambda e: e.tensor_scalar(out=out, in0=in0, scalar1=s1, scalar2=None,
                                                                  op0=op0), rd=rd, wr=wr)
                return tk.op("dve", lambda e: e.tensor_scalar(out=out, in0=in0, scalar1=s1, scalar2=s2,
                                                              op0=op0, op1=op1), rd=rd, wr=wr)

            def stt(out, in0, scalar, in1, op0, op1, rd, wr):
                return tk.op("dve", lambda e: e.scalar_tensor_tensor(out=out, in0=in0, scalar=scalar, in1=in1,
                                                                     op0=op0, op1=op1), rd=rd, wr=wr)

            def recip(out, in_, rd, wr):
                return tk.op("dve", lambda e: e.reciprocal(out=out, in_=in_), rd=rd, wr=wr)

            def dcopy(out, in_, rd, wr):
                return tk.op("dve", lambda e: e.tensor_copy(out=out, in_=in_), rd=rd, wr=wr)

            def acopy(out, in_, rd, wr):
                return tk.op("act", lambda e: e.activation(out=out, in_=in_, func=AF.Copy), rd=rd, wr=wr)

            def dbg(name, ap, bufs, shape, dt):
                if not debug:
                    return
                d = nc.dram_tensor("dbg_" + name, list(shape), dt, kind="ExternalOutput").ap()
                b = Buf("dbg_" + name)
                dbg_d[name] = b
                tk.op("pool", lambda e: e.dma_start(out=d, in_=ap), rd=bufs, wr=[b], dkey="dbg_" + name)

            EPSC = MISC[:, 0:1]
            ONEC = MISC[:, 1:2]
            OML = MISC[:, 2:10]
            LBD = MISC[:, 10:18]

            tk.op("sp", lambda e: e.dma_start(out=CST[:], in_=consts_d[:, :]), wr=[b_CST], dkey="i0")
            tk.op("sp", lambda e: e.dma_start(out=MASK[:], in_=mask_d[:, :]), wr=[b_MASK], dkey="i1")
            tk.op("sp", lambda e: e.dma_start(out=RESET[:], in_=reset_d[:, :]), wr=[b_RESET], dkey="i2")
            tk.op("sp", lambda e: e.dma_start(out=IDENT[:], in_=ident_d[:, :]), wr=[b_ID], dkey="i3")
            tk.op("sp", lambda e: e.dma_start(out=ONES[:], in_=ones_d[:, :]), wr=[b_ONES], dkey="i4")
            tk.op("dve", lambda e: e.memset(MISC[:, 0:1], EPS), wr=[b_MISC])
            tk.op("dve", lambda e: e.memset(MISC[:, 1:2], 1.0), wr=[b_MISC])
            tk.op("dve", lambda e: e.memset(SS_[:], 0.0), wr=b_S)
            tk.op("dve", lambda e: e.memset(SB16[:], 0.0), wr=b_S16)
            tt(LBD, CST[:, C_LB1:C_LB1 + 8], CST[:, C_LB0:C_LB0 + 8], ALU.subtract, rd=[b_CST, b_MISC], wr=[b_MISC])
            act(OML, LBD, AF.Sigmoid, rd=[b_MISC], wr=[b_MISC])

            pidx = {s[0]: k for k, s in enumerate(specs)}
            seq = [s[0] for s in specs]
            wstate = {"issued": 0, "cur": 0, "slots": {}}
            total_pieces = NT * len(seq)
            PF = NW - 1

            def w_issue(n):
                name = seq[n % len(seq)]
                pi = pidx[name]
                size = specs[pi][1]
                ap, b = wring.items[n % NW]
                tk.op("sp", lambda e: e.dma_start(out=ap[:, 0:size], in_=wp_d[pi, :, 0:size]),
                      wr=[b], dkey=f"w{n % NW}")

            def wget(expect):
                if stages < 9:
                    while seq[wstate["cur"] % len(seq)] != expect:
                        wstate["cur"] += 1
                        wstate["issued"] = max(wstate["issued"], wstate["cur"])
                n = wstate["cur"]
                assert seq[n % len(seq)] == expect, (seq[n % len(seq)], expect)
                while wstate["issued"] < min(n + PF, total_pieces):
                    w_issue(wstate["issued"])
                    wstate["issued"] += 1
                wstate["cur"] += 1
                return wring.items[n % NW]

            cseq = [(i, h, j) for i in range(NT) for h in range(NH) for j in range(i)]
            cstate = {"issued": 0, "cur": 0}
            CPF = NC_ - 1

            def c_issue(n):
                i, h, j = cseq[n]
                ap, b = cring.items[n % NC_]
                tk.op("sp", lambda e: e.dma_start(out=ap[:, :], in_=kvc_d[j, :, 1536 * h:1536 * h + 1536]),
                      rd=[b_KVC[j]], wr=[b], dkey=f"c{n % NC_}")

            def cget(i, h, j):
                n = cstate["cur"]
                assert cseq[n] == (i, h, j)
                while cstate["issued"] < min(n + CPF, len(cseq)) and (
                        cstate["issued"] <= n or cseq[cstate["issued"]][0] <= i):
                    c_issue(cstate["issued"])
                    cstate["issued"] += 1
                cstate["cur"] += 1
                return cring.items[n % NC_]

            def _rms_rb_unused(srcs, nfeat):
                pa, pb = psr.get()
                n = len(srcs)
                for q, (ap, bufs, npart) in enumerate(srcs):
                    sa, sbf = sqr.get()
                    act(sa[0:npart, :], ap, AF.Square, rd=bufs, wr=[sbf])
                    mm1(pa[:, :], ONES[0:npart, :], sa[0:npart, :], q == 0, q == n - 1,
                        rd=[b_ONES, sbf], wr=[pb])
                ra, rbf = tmpr.get()
                act(ra[:, :], pa[:, :], AF.Sqrt, rd=[pb, b_MISC], wr=[rbf], scale=1.0 / nfeat, bias=EPSC)
                r2, r2b = tmpr.get()
                recip(r2[:, :], ra[:, :], rd=[rbf], wr=[r2b])
                return r2, r2b

            def norm_to_xn():
                rb, rbb = rms_rb([(XT[:, c, :], [b_XT[c]], 128) for c in range(8)], D)
                for c in range(8):
                    tt(XN[:, c, :], XT[:, c, :], rb[:, :], ALU.mult, rd=[b_XT[c], rbb], wr=[b_XN[c]])

            def ffn(pfx):
                norm_to_xn()
                for j2 in range(11):
                    wa, wb = wget(f"{pfx}_wi{j2}")
                    for jj in range(2):
                        j = 2 * j2 + jj
                        ga, gb = psr.get()
                        ua, ub = psr.get()
                        mm(ga[:, :], [(wa[:, k * 256 + jj * 128:k * 256 + jj * 128 + 128], XN[:, k, :])
                                      for k in range(8)], rd=[wb] + b_XN, wr=[gb])
                        mm(ua[:, :], [(wa[:, 2048 + k * 256 + jj * 128:2048 + k * 256 + jj * 128 + 128], XN[:, k, :])
                                      for k in range(8)], rd=[wb] + b_XN, wr=[ub])
                        sa, sbf = tmpr.get()
                        act(sa[:, :], ga[:, :], AF.Silu, rd=[gb], wr=[sbf])
                        tt(HID[:, j, :], sa[:, :], ua[:, :], ALU.mult, rd=[sbf, ub], wr=[b_HID[j]])
                for m in range(8):
                    wa, wb = wget(f"{pfx}_wo{m}")
                    pa, pb = psr.get()
                    mm(pa[:, :], [(wa[:, k * 128:k * 128 + 128], HID[:, k, :]) for k in range(22)],
                       rd=[wb] + b_HID, wr=[pb])
                    stt(XT[:, m, :], pa[:, :], 0.5, XT[:, m, :], ALU.mult, ALU.add,
                        rd=[pb, b_XT[m]], wr=[b_XT[m]])

            def rope_tables(i):
                c0 = i * T
                tk.op("sp", lambda e: e.dma_start(out=POSI[0:64, :],
                                                  in_=pos_d[0:1, c0:c0 + T].partition_broadcast(64)),
                      wr=[b_POSI], dkey="pos")
                pf, pfb = tmpr.get()
                dcopy(pf[0:64, :], POSI[0:64, :], rd=[b_POSI], wr=[pfb])
                ang, angb = tmpr.get()
                ts(ang[0:64, :], pf[0:64, :], CST[0:64, C_INVF:C_INVF + 1], ALU.mult, rd=[pfb, b_CST], wr=[angb])
                r, rb_ = tmpr.get()
                ts(r[0:64, :], ang[0:64, :], MAGIC, ALU.add, rd=[angb], wr=[rb_], s2=MAGIC, op1=ALU.subtract)
                fr, frb = tmpr.get()
                tt(fr[0:64, :], ang[0:64, :], r[0:64, :], ALU.subtract, rd=[angb, rb_], wr=[frb])
                act(NS[0:64, :], fr[0:64, :], AF.Sin, rd=[frb, b_CST], wr=[b_NS],
                    scale=CST[0:64, C_S2PI:C_S2PI + 1])
                a2, a2b = tmpr.get()
                ts(a2[0:64, :], ang[0:64, :], 0.25, ALU.add, rd=[angb], wr=[a2b])
                r2, r2b = tmpr.get()
                ts(r2[0:64, :], a2[0:64, :], MAGIC, ALU.add, rd=[a2b], wr=[r2b], s2=MAGIC, op1=ALU.subtract)
                f2, f2b = tmpr.get()
                tt(f2[0:64, :], a2[0:64, :], r2[0:64, :], ALU.subtract, rd=[a2b, r2b], wr=[f2b])
                act(CC[0:64, :], f2[0:64, :], AF.Sin, rd=[f2b], wr=[b_CC], scale=TWO_PI)

            def rope(src, srcb, dst, dstb):
                t2, t2b = tmpr.get()
                tt(dst[0:64, :], src[0:64, :], CC[0:64, :], ALU.mult, rd=[srcb, b_CC], wr=[dstb])
                tt(t2[0:32, :], src[32:64, :], NS[32:64, :], ALU.mult, rd=[srcb, b_NS], wr=[t2b])
                tt(t2[32:64, :], src[0:32, :], NS[0:32, :], ALU.mult, rd=[srcb, b_NS, t2b], wr=[t2b])
                tt(dst[0:64, :], dst[0:64, :], t2[0:64, :], ALU.add, rd=[dstb, t2b], wr=[dstb])

            def hgrn_group(g):
                wa, wb = wget(f"hgi{g}")
                for tb in range(4):
                    pa, pb = psr.get()
                    mm(pa[:, :], [(XN[:, k, 128 * tb:128 * tb + 128], wa[:, k * 512:k * 512 + 512])
                                  for k in range(8)], rd=[wb] + b_XN, wr=[pb])
                    acopy(VTOK[:, tb, :], pa[:, :], rd=[pb], wr=[b_VT[tb]])
                wa, wb = wget(f"hgf{g}")
                for hh in range(4):
                    h = 4 * g + hh
                    za, zb = psr.get()
                    mm(za[:, :], [(wa[:, k * 512 + hh * 128:k * 512 + hh * 128 + 128], XN[:, k, :])
                                  for k in range(8)], rd=[wb] + b_XN, wr=[zb])
                    sg, sgb = tmpr.get()
                    act(sg[:, :], za[:, :], AF.Sigmoid, rd=[zb], wr=[sgb], scale=-1.0)
                    kk, kkb = tmpr.get()
                    ts(kk[:, :], sg[:, :], OML[:, h:h + 1], ALU.mult, rd=[sgb, b_MISC], wr=[kkb])
                    lf, lfb = tmpr.get()
                    act(lf[:, :], kk[:, :], AF.Ln, rd=[kkb, b_MISC], wr=[lfb], scale=-1.0, bias=ONEC)
                    cu, cub = tmpr.get()
                    tk.op("dve", lambda e, cu=cu, lf=lf: e.tensor_tensor_scan(
                        out=cu[:, :], data0=RESET[:, :], data1=lf[:, :], initial=0.0,
                        op0=ALU.mult, op1=ALU.add), rd=[lfb, b_RESET], wr=[cub])
                    ts(cu[:, :], cu[:, :], -80.0, ALU.max, rd=[cub], wr=[cub])
                    act(EC4[:, hh, :], cu[:, :], AF.Exp, rd=[cub], wr=[b_EC[hh]])
                    dcopy(EL[:, h, :], EC4[:, hh, 63:T:64], rd=[b_EC[hh]], wr=[b_EL[h]])
                    en, enb = tmpr.get()
                    act(en[:, :], cu[:, :], AF.Exp, rd=[cub], wr=[enb], scale=-1.0)
                    tt(KT[:, hh, :], kk[:, :], en[:, :], ALU.mult, rd=[kkb, enb], wr=[b_KT[hh]])
                wa, wb = wget(f"hgq{g}")
                for hh in range(4):
                    qa, qb = psr.get()
                    mm(qa[:, :], [(wa[:, k * 512 + hh * 128:k * 512 + hh * 128 + 128], XN[:, k, :])
                                  for k in range(8)], rd=[wb] + b_XN, wr=[qb])
                    sq_, sqb_ = tmpr.get()
                    act(sq_[:, :], qa[:, :], AF.Silu, rd=[qb], wr=[sqb_])
                    tt(QT[:, hh, :], sq_[:, :], EC4[:, hh, :], ALU.mult, rd=[sqb_, b_EC[hh]], wr=[b_QT[hh]])
                wa, wb = wget(f"hgg{g}")
                for hh in range(4):
                    qa, qb = psr.get()
                    mm(qa[:, :], [(wa[:, k * 512 + hh * 128:k * 512 + hh * 128 + 128], XN[:, k, :])
                                  for k in range(8)], rd=[wb] + b_XN, wr=[qb])
                    act(G[:, hh, :], qa[:, :], AF.Silu, rd=[qb], wr=[b_G[hh]])
                for b in range(4):
                    cs = slice(128 * b, 128 * b + 128)
                    atm_l, ktok_l = [], []
                    for hh in range(4):
                        aa, ab = psq.get()
                        mm(aa, [(KT[:, hh, cs], QT[:, hh, cs])], rd=[b_KT[hh], b_QT[hh]], wr=[ab])
                        ma, mb = atmr.get()
                        tt(ma[:, :], aa, MASK[:, :], ALU.mult, rd=[ab, b_MASK], wr=[mb])
                        ta, tb_ = psb.get()
                        tk.op("pe", lambda e, ta=ta, hh=hh, cs=cs: e.transpose(ta, KT[:, hh, cs], IDENT[:, :]),
                              rd=[b_KT[hh], b_ID], wr=[tb_])
                        ka, kb = ktokr.get()
                        acopy(ka[:, :], ta, rd=[tb_], wr=[kb])
                        atm_l.append((ma, mb))
                        ktok_l.append((ka, kb))
                    for cc in range(2):
                        c = 2 * b + cc
                        c64 = slice(128 * b + 64 * cc, 128 * b + 64 * cc + 64)
                        for hh in range(4):
                            h = 4 * g + hh
                            if cc == 0:
                                mm1(PS[hh][:, cs], VTOK[:, b, hh * 128:hh * 128 + 128], atm_l[hh][0][:, :],
                                    True, False, rd=[b_VT[b], atm_l[hh][1]], wr=[b_PS[hh]])
                            mm1(PS[hh][:, c64], SB16[:, h, :], QT[:, hh, c64], False, cc == 1,
                                rd=[b_S16[h], b_QT[hh]], wr=[b_PS[hh]])
                            sa, sbf = psq.get()
                            ka, kb = ktok_l[hh]
                            mm(sa, [(ka[64 * cc:64 * cc + 64, :],
                                     VTOK[64 * cc:64 * cc + 64, b, hh * 128:hh * 128 + 128])],
                               rd=[kb, b_VT[b]], wr=[sbf])
                            el = EL[:, h, c:c + 1]
                            ts(SS_[:, h, :], SS_[:, h, :], el, ALU.mult, rd=[b_S[h], b_EL[h]], wr=[b_S[h]])
                            stt(SS_[:, h, :], sa, el, SS_[:, h, :], ALU.mult, ALU.add,
                                rd=[sbf, b_EL[h], b_S[h]], wr=[b_S[h]])
                            acopy(SB16[:, h, :], SS_[:, h, :], rd=[b_S[h]], wr=[b_S16[h]])
                for hh in range(4):
                    h = 4 * g + hh
                    sa, sbf = sqr.get()
                    act(sa[:, :], PS[hh][:, :], AF.Square, rd=[b_PS[hh]], wr=[sbf])
                    pa, pb = PS[4 + hh % 2], b_PS[4 + hh % 2]
                    mm(pa[:, :], [(ONES[:, :], sa[:, :])], rd=[b_ONES, sbf], wr=[pb])
                    ra, rbf = tmpr.get()
                    act(ra[:, :], pa[:, :], AF.Sqrt, rd=[pb, b_MISC], wr=[rbf], scale=1.0 / 128, bias=EPSC)
                    r2, r2b = tmpr.get()
                    recip(r2[:, :], ra[:, :], rd=[rbf], wr=[r2b])
                    y, yb = tmpr.get()
                    tt(y[:, :], PS[hh][:, :], r2[:, :], ALU.mult, rd=[b_PS[hh], r2b], wr=[yb])
                    tt(YH[:, h, :], y[:, :], G[:, hh, :], ALU.mult, rd=[yb, b_G[hh]], wr=[b_YH[h]])

            def mla_proj(i):
                rope_tables(i)
                wa, wb = wget("cq")
                pq = [psr.get() for _ in range(3)]
                for c in range(3):
                    mm(pq[c][0][:, :], [(wa[:, k * 384 + c * 128:k * 384 + c * 128 + 128], XN[:, k, :])
                                        for k in range(8)], rd=[wb] + b_XN, wr=[pq[c][1]])
                pk, pkb = psr.get()
                mm(pk[0:64, :], [(wa[:, 3072 + k * 64:3072 + k * 64 + 64], XN[:, k, :]) for k in range(8)],
                   rd=[wb] + b_XN, wr=[pkb])
                rb, rbb = rms_rb([(pq[c][0][:, :], [pq[c][1]], 128) for c in range(3)], 384)
                for c in range(3):
                    tt(CQN[:, c, :], pq[c][0][:, :], rb[:, :], ALU.mult, rd=[pq[c][1], rbb], wr=[b_CQN[c]])
                act(SQPE[0:64, :], pk[0:64, :], AF.Square, rd=[pkb], wr=[b_SQPE])
                kg, kgb = tmpr.get()
                ts(kg[0:64, :], pk[0:64, :], CST[0:64, C_GKR:C_GKR + 1], ALU.mult, rd=[pkb, b_CST], wr=[kgb])
                rope(kg, kgb, KPR, b_KPR)
                wa, wb = wget("ckv")
                pc = [psr.get() for _ in range(2)]
                for c in range(2):
                    mm(pc[c][0][:, :], [(wa[:, k * 256 + c * 128:k * 256 + c * 128 + 128], XN[:, k, :])
                                        for k in range(8)], rd=[wb] + b_XN, wr=[pc[c][1]])
                rb, rbb = rms_rb([(pc[c][0][:, :], [pc[c][1]], 128) for c in range(2)], 256)
                for c in range(2):
                    tt(CKVN[:, c, :], pc[c][0][:, :], rb[:, :], ALU.mult, rd=[pc[c][1], rbb], wr=[b_CKVN[c]])
                for h in range(NH):
                    hh = h % 4
                    if hh == 0:
                        wa, wb = wget(f"qup{h // 4}")
                    qn, qnb = psr.get()
                    qr, qrb = psr.get()
                    o0 = hh * 192
                    mm(qn[:, :], [(wa[:, k * 768 + o0:k * 768 + o0 + 128], CQN[:, k, :]) for k in range(3)],
                       rd=[wb] + b_CQN, wr=[qnb])
                    mm(qr[0:64, :], [(wa[:, k * 768 + o0 + 128:k * 768 + o0 + 192], CQN[:, k, :]) for k in range(3)],
                       rd=[wb] + b_CQN, wr=[qrb])
                    rq, rqb = rms_rb([(qn[:, :], [qnb], 128), (qr[0:64, :], [qrb], 64)], 192)
                    stt(QM[:, h, :], qn[:, :], CST[:, C_GQN:C_GQN + 1], rq[:, :], ALU.mult, ALU.mult,
                        rd=[qnb, b_CST, rqb], wr=[b_QM[h]])
                    qg, qgb = tmpr.get()
                    ts(qg[0:64, :], qr[0:64, :], CST[0:64, C_GQR:C_GQR + 1], ALU.mult, rd=[qrb, b_CST], wr=[qgb])
                    t1, t1b = tmpr.get()
                    rope(qg, qgb, t1, t1b)
                    tt(QRT[0:64, h, :], t1[0:64, :], rq[0:64, :], ALU.mult, rd=[t1b, rqb], wr=[b_QRT[h]])
                wa, wb = wget("kvk")
                for h in range(NH):
                    kn, knb = psr.get()
                    mm(kn[:, :], [(wa[:, h * 256 + k * 128:h * 256 + k * 128 + 128], CKVN[:, k, :]) for k in range(2)],
                       rd=[wb] + b_CKVN, wr=[knb])
                    rk, rkb = rms_rb([(kn[:, :], [knb], 128), (None, None, 64)], 192)
                    stt(CU[:, h, 0:512], kn[:, :], CST[:, C_GKN:C_GKN + 1], rk[:, :], ALU.mult, ALU.mult,
                        rd=[knb, b_CST, rkb], wr=b_CU(h))
                    tt(CU[0:64, h, 512:1024], KPR[0:64, :], rk[0:64, :], ALU.mult,
                       rd=[b_KPR, rkb] + b_CU(h), wr=b_CU(h))
                wa, wb = wget("kvv")
                for tb in range(4):
                    for half in range(2):
                        pv, pvb = psr.get()
                        mm(pv[:, :], [(CKVN[:, k, 128 * tb:128 * tb + 128],
                                       wa[:, k * 1024 + half * 512:k * 1024 + half * 512 + 512]) for k in range(2)],
                           rd=[wb] + b_CKVN, wr=[pvb])
                        dst = CU[:, 4 * half:4 * half + 4, 1024 + 128 * tb:1024 + 128 * tb + 128]
                        src = pv[:, :].rearrange("p (h d) -> p h d", d=128)
                        cb = []
                        for h in range(4 * half, 4 * half + 4):
                            cb += b_CU(h)
                        acopy(dst, src, rd=[pvb] + cb, wr=cb)
                tk.op("pool", lambda e: e.dma_start(out=kvc_d[i, :, :], in_=AR1[:, :]),
                      rd=b_AR, wr=[b_KVC[i]], dkey="cs")


            def rms_rb(srcs, nfeat):
                pa, pb = psr.get()
                n = len(srcs)
                for q, (ap, bufs, npart) in enumerate(srcs):
                    if ap is None:
                        mm1(pa[:, :], ONES[0:npart, :], SQPE[0:npart, :], q == 0, q == n - 1,
                            rd=[b_ONES, b_SQPE], wr=[pb])
                        continue
                    sa, sbf = sqr.get()
                    act(sa[0:npart, :], ap, AF.Square, rd=bufs, wr=[sbf])
                    mm1(pa[:, :], ONES[0:npart, :], sa[0:npart, :], q == 0, q == n - 1,
                        rd=[b_ONES, sbf], wr=[pb])
                ra, rbf = tmpr.get()
                act(ra[:, :], pa[:, :], AF.Sqrt, rd=[pb, b_MISC], wr=[rbf], scale=1.0 / nfeat, bias=EPSC)
                r2, r2b = tmpr.get()
                recip(r2[:, :], ra[:, :], rd=[rbf], wr=[r2b])
                return r2, r2b

            def attention(i):
                sidx = [0]
                for h in range(NH):
                    Ob, Obb = PS[2 + h % 2], b_PS[2 + h % 2]
                    Db, Dbb = PS[4 + h % 2], b_PS[4 + h % 2]
                    nblk = 4 * i + 4
                    cnt = [0]

                    def blk(kn, kr, v, q0, diag, kbufs):
                        Sb, Sbb = PS[sidx[0]], b_PS[sidx[0]]
                        sidx[0] ^= 1
                        first = cnt[0] == 0
                        last = cnt[0] == nblk - 1
                        cnt[0] += 1
                        mm(Sb[:, q0:T], [(kn, QM[:, h, q0:T]), (kr, QRT[0:64, h, q0:T])],
                           rd=kbufs + [b_QM[h], b_QRT[h]], wr=[Sbb])
                        pt, ptb = ptr.get()
                        act(pt[:, q0:T], Sb[:, q0:T], AF.Exp, rd=[Sbb], wr=[ptb], scale=SCALE)
                        if diag:
                            tk.op("dve", lambda e: e.memset(pt[64:128, q0:q0 + 64], 0.0), rd=[ptb], wr=[ptb])
                        mm1(Ob[:, q0:T], v, pt[:, q0:T], first, last, rd=kbufs + [ptb], wr=[Obb])
                        mm1(Db[:, q0:T], ONES[:, :], pt[:, q0:T], first, last, rd=[b_ONES, ptb], wr=[Dbb])

                    for j in range(i):
                        ca, cb = cget(i, h, j)
                        for kb in range(4):
                            blk(ca[:, 128 * kb:128 * kb + 128], ca[0:64, 512 + 128 * kb:512 + 128 * kb + 128],
                                ca[:, 1024 + 128 * kb:1024 + 128 * kb + 128], 0, False, [cb])
                    for kb in range(4):
                        blk(CU[:, h, 128 * kb:128 * kb + 128], CU[0:64, h, 512 + 128 * kb:512 + 128 * kb + 128],
                            CU[:, h, 1024 + 128 * kb:1024 + 128 * kb + 128], 128 * kb, True, b_CU(h))
                    rd_, rdb = tmpr.get()
                    recip(rd_[:, :], Db[:, :], rd=[Dbb], wr=[rdb])
                    tt(ATT[:, h, :], Ob[:, :], rd_[:, :], ALU.mult, rd=[Obb, rdb], wr=[b_ATT[h]])

            def merge_out():
                for m in range(8):
                    wa, wb = wget(f"mix{m}")
                    A, Ab = psr.get()
                    B, Bb = psr.get()
                    C, Cb = psr.get()
                    Dp, Dpb = psr.get()
                    mm(A[:, :], [(wa[:, k * 128:k * 128 + 128], YH[:, k, :]) for k in range(8)],
                       rd=[wb] + b_YH, wr=[Ab])
                    mm(B[:, :], [(wa[:, 1024 + k * 128:1024 + k * 128 + 128], XN[:, k, :]) for k in range(8)],
                       rd=[wb] + b_XN, wr=[Bb])
                    mm(C[:, :], [(wa[:, 2048 + k * 128:2048 + k * 128 + 128], ATT[:, k, :]) for k in range(8)],
                       rd=[wb] + b_ATT, wr=[Cb])
                    mm(Dp[:, :], [(wa[:, 3072 + k * 128:3072 + k * 128 + 128], XN[:, k, :]) for k in range(8)],
                       rd=[wb] + b_XN, wr=[Dpb])
                    g1, g1b = tmpr.get()
                    act(g1[:, :], B[:, :], AF.Sigmoid, rd=[Bb, b_CST], wr=[g1b], bias=CST[:, C_BM + m:C_BM + m + 1])
                    g2, g2b = tmpr.get()
                    act(g2[:, :], Dp[:, :], AF.Sigmoid, rd=[Dpb, b_CST], wr=[g2b],
                        bias=CST[:, C_BM + 8 + m:C_BM + 8 + m + 1])
                    t1, t1b = tmpr.get()
                    tt(t1[:, :], A[:, :], g1[:, :], ALU.mult, rd=[Ab, g1b], wr=[t1b])
                    t2, t2b = tmpr.get()
                    tt(t2[:, :], C[:, :], g2[:, :], ALU.mult, rd=[Cb, g2b], wr=[t2b])
                    tt(QM[:, m, :], t1[:, :], t2[:, :], ALU.add, rd=[t1b, t2b], wr=[b_QM[m]])
                for m in range(8):
                    if m % 4 == 0:
                        wa, wb = wget(f"wout{m // 4}")
                    pa, pb = psr.get()
                    mm(pa[:, :], [(wa[:, k * 512 + (m % 4) * 128:k * 512 + (m % 4) * 128 + 128], QM[:, k, :])
                                  for k in range(8)], rd=[wb] + b_QM, wr=[pb])
                    tt(XT[:, m, :], pa[:, :], XT[:, m, :], ALU.add, rd=[pb, b_XT[m]], wr=[b_XT[m]])

            for i in range(NT):
                c0 = i * T
                tk.op("sp", lambda e, c0=c0: e.dma_start(
                    out=XT[:, :, :], in_=xT_d[:, c0:c0 + T].rearrange("(c p) t -> p c t", p=128)),
                    wr=b_XT, dkey="x")
                if stages >= 1:
                    ffn("ffn1")
                if debug and i == 0:
                    dbg("h1", XT[:, :, :], b_XT, [128, 8, T], F32)
                if stages >= 2:
                    norm_to_xn()
                    hgrn_group(0)
                    hgrn_group(1)
                if debug and i == NT - 1:
                    dbg("yh", YH[:, :, :], b_YH, [128, 8, T], BF16)
                if stages >= 3:
                    mla_proj(i)
                if debug and i == NT - 1:
                    dbg("qm", QM[:, :, :], b_QM, [128, 8, T], BF16)
                    dbg("qrt", QRT[:, :, :], b_QRT, [128, 8, T], BF16)
                    dbg("cu", AR1[:, :], b_AR, [128, 24 * 512], BF16)
                if stages >= 4:
                    attention(i)
                if debug and i == NT - 1:
                    dbg("att", ATT[:, :, :], b_ATT, [128, 8, T], BF16)
                if stages >= 5:
                    merge_out()
                if debug and i == NT - 1:
                    dbg("h2", XT[:, :, :], b_XT, [128, 8, T], F32)
                if stages >= 6:
                    ffn("ffn2")
                rb, rbb = rms_rb([(XT[:, c, :], [b_XT[c]], 128) for c in range(8)], D)
                for c in range(8):
                    stt(XT[:, c, :], XT[:, c, :], CST[:, C_FIN + c:C_FIN + c + 1], rb[:, :], ALU.mult, ALU.mult,
                        rd=[b_XT[c], b_CST, rbb], wr=[b_XT[c]])
                tk.op("pool", lambda e, c0=c0: e.dma_start(
                    out=outT_d[:, c0:c0 + T].rearrange("(c p) t -> p c t", p=128), in_=XT[:, :, :]),
                    rd=b_XT, wr=[b_OUT[i]], dkey="o")
            tk.op("pool", None, rd=b_OUT + list(dbg_d.values()))
            tk.op("sp", None, rd=b_OUT + list(dbg_d.values()))
            tk.finalize()
            tk.alloc_sems(semstack)
            tk.run_block()
    return nc


def _pack_consts(inp):
    c = np.zeros((128, NCOL), np.float32)

    def put(col, vec):
        v = np.asarray(vec, np.float32).reshape(-1)
        n = (v.size + 127) // 128
        for k in range(n):
            seg = v[k * 128:(k + 1) * 128]
            c[:seg.size, col + k] = seg

    put(C_FFN1, inp["ffn1_norm"][0])
    put(C_MIX, inp["mix_norm"][0])
    put(C_FFN2, inp["ffn2_norm"][0])
    put(C_FIN, inp["final_norm"][0])
    put(C_QL, inp["mla_q_lora_norm"][0])
    put(C_KVL, inp["mla_kv_lora_norm"][0])
    put(C_HGO, inp["hg_out_norm"][0])
    put(C_GQN, inp["q_head_norm"][0][:128])
    put(C_GQR, inp["q_head_norm"][0][128:])
    put(C_GKN, inp["k_head_norm"][0][:128])
    put(C_GKR, inp["k_head_norm"][0][128:])
    put(C_BM, inp["b_merge"][0])
    put(C_LB0, inp["hg_lb_table"][0])
    put(C_LB1, inp["hg_lb_table"][1])
    invf = (10000.0 ** (-np.arange(0, 64, 2, dtype=np.float32) / 64)).astype(np.float32)
    c[:64, C_INVF] = np.concatenate([invf, invf]) / np.float32(2 * math.pi)
    c[:32, C_S2PI] = TWO_PI
    c[32:64, C_S2PI] = -TWO_PI
    return c


def _const_tables():
    s = np.arange(128)[:, None]
    t = np.arange(128)[None, :]
    mask = ((s // 64 == t // 64) & (s <= t)).astype(np.float32)
    reset = np.ones((128, T), np.float32)
    reset[:, ::64] = 0.0
    ident = np.eye(128, dtype=np.float32).astype(ml_dtypes.bfloat16)
    ones = np.ones((128, 128), np.float32).astype(ml_dtypes.bfloat16)
    return mask, reset, ident, ones


def make_in_maps(inp, nb, S):
    consts = _pack_consts(inp)
    mask, reset, ident, ones = _const_tables()
    ws = {n: np.ascontiguousarray(np.asarray(inp[n])[0], dtype=np.float32) for n in WSHAPES}
    maps = []
    for b in range(nb):
        m = {"xT": np.ascontiguousarray(np.asarray(inp["x"])[b, :S].T, dtype=np.float32),
             "pos": np.ascontiguousarray(np.asarray(inp["positions"])[b:b + 1, :S], dtype=np.int32),
             "consts": consts, "mask128": mask, "resetm": reset, "ident": ident, "onesb": ones}
        m.update(ws)
        maps.append(m)
    return maps


def kernel(**inputs):
    x = np.asarray(inputs["x"])
    B, S, _ = x.shape
    nc = build(S // T)
    maps = make_in_maps(inputs, B, S)
    res = run_bass_kernel_spmd(nc, maps, core_ids=list(range(B)))
    out = np.stack([np.asarray(r["outT"]).T for r in res.results], axis=0)
    return np.ascontiguousarray(out.astype(np.float32))
```

```python
import math
from contextlib import ExitStack

import numpy as np
import ml_dtypes
import concourse.bass as bass
import concourse.mybir as mybir
from concourse.bass_utils import run_bass_kernel_spmd

F32 = mybir.dt.float32
BF16 = mybir.dt.bfloat16
I32 = mybir.dt.int32
AF = mybir.ActivationFunctionType
ALU = mybir.AluOpType

D = 1024
DFF = 2816
T = 512
NH = 8
EPS = 1e-6
SCALE = 192.0 ** -0.5
MAGIC = 12582912.0
TWO_PI = 6.28318
ENGS = ["pe", "act", "dve", "pool", "sp"]
SEM_CAP = 2000
DSEM_CAP = 100
SAME_WIN = 3

C_FFN1, C_MIX, C_FFN2, C_FIN = 0, 8, 16, 24
C_QL, C_KVL, C_HGO = 32, 35, 37
C_GQN, C_GQR, C_GKN, C_GKR = 38, 39, 40, 41
C_BM = 42
C_LB0, C_LB1 = 58, 66
C_INVF, C_S2PI = 74, 75
NCOL = 76


class Buf:
    __slots__ = ("name", "lw", "rd", "rdd", "excl")

    def __init__(self, name, excl=False):
        self.name = name
        self.excl = excl
        self.lw = None
        self.rd = {}
        self.rdd = []


class Op:
    __slots__ = ("eng", "fn", "deps", "idx", "pub", "ticket", "dkey", "dval", "waits")


class Tracker:
    def __init__(self, nc, tag):
        self.nc = nc
        self.tag = tag
        self.ops = {e: [] for e in ENGS}
        self.dcount = {}

    def op(self, eng, fn, rd=(), wr=(), dkey=None):
        o = Op()
        o.eng, o.fn, o.dkey = eng, fn, dkey
        o.pub, o.ticket, o.waits = False, 0, None
        o.idx = len(self.ops[eng])
        if dkey is not None:
            self.dcount[dkey] = self.dcount.get(dkey, 0) + 1
            o.dval = self.dcount[dkey]
        else:
            o.dval = 0
        deps = {}
        ex = [b for b in rd if b.excl and b not in wr]
        if ex:
            wr = list(wr) + ex

        def add(d):
            if d is None or d is o:
                return
            if d.dkey is not None:
                deps[("d", d.dkey, d.dval)] = d
            else:
                k = ("e", d.eng)
                c = deps.get(k)
                if c is None or c.idx < d.idx:
                    deps[k] = d

        for b in rd:
            add(b.lw)
        for b in wr:
            add(b.lw)
            for r in b.rd.values():
                add(r)
            for r in b.rdd:
                add(r)
        o.deps = list(deps.values())
        for b in rd:
            if dkey is not None:
                b.rdd.append(o)
            else:
                b.rd[eng] = o
        for b in wr:
            b.lw = o
            b.rd = {}
            b.rdd = []
        self.ops[eng].append(o)
        return o

    def finalize(self):
        for e in ENGS:
            seen = {}
            for o in self.ops[e]:
                need = {}
                for d in o.deps:
                    if d.dkey is not None:
                        k = ("d", d.dkey)
                        if seen.get(k, 0) < d.dval:
                            c = need.get(k)
                            if c is None or c.dval < d.dval:
                                need[k] = d
                    else:
                        if d.eng == e:
                            if e == "pe" or e == "sp":
                                continue
                            if o.idx - d.idx > SAME_WIN:
                                continue
                        k = ("e", d.eng)
                        if seen.get(k, -1) < d.idx:
                            c = need.get(k)
                            if c is None or c.idx < d.idx:
                                need[k] = d
                for k, d in need.items():
                    if k[0] == "d":
                        seen[k] = d.dval
                    else:
                        seen[k] = d.idx
                        d.pub = True
                o.waits = list(need.values())
        self.nsem = {}
        for e in ENGS:
            t = 0
            for o in self.ops[e]:
                if o.dkey is None and o.pub:
                    t += 1
                    o.ticket = t
            self.nsem[e] = (t + SEM_CAP - 1) // SEM_CAP

    def alloc_sems(self, stack):
        nc = self.nc
        self.esems = {}
        for e in ENGS:
            self.esems[e] = [stack.enter_context(nc.semaphore(f"{self.tag}_{e}{k}"))
                             for k in range(self.nsem[e])]
        self.dsems = {}
        for key, n in self.dcount.items():
            ng = (n + DSEM_CAP - 1) // DSEM_CAP
            self.dsems[key] = [stack.enter_context(nc.semaphore(f"{self.tag}_d{key}_{g}"))
                               for g in range(ng)]

    def _esem(self, e, ticket):
        return self.esems[e][(ticket - 1) // SEM_CAP], (ticket - 1) % SEM_CAP + 1

    def _dsem(self, key, n):
        return self.dsems[key][(n - 1) // DSEM_CAP], 16 * ((n - 1) % DSEM_CAP + 1)

    def emit_engine(self, e, eng):
        for o in self.ops[e]:
            for d in o.waits:
                if d.dkey is not None:
                    s, v = self._dsem(d.dkey, d.dval)
                else:
                    s, v = self._esem(d.eng, d.ticket)
                eng.wait_ge(s, v)
            if o.fn is None:
                continue
            inst = o.fn(eng)
            if o.dkey is not None:
                s, v = self._dsem(o.dkey, o.dval)
                inst.then_inc(s, 16)
            elif o.pub:
                s, v = self._esem(e, o.ticket)
                inst.then_inc(s, 1)

    def run_block(self):
        nc = self.nc
        with nc.Block() as block:
            @block.sync
            def _(eng):
                self.emit_engine("sp", eng)

            @block.tensor
            def _(eng):
                self.emit_engine("pe", eng)

            @block.scalar
            def _(eng):
                self.emit_engine("act", eng)

            @block.vector
            def _(eng):
                self.emit_engine("dve", eng)

            @block.gpsimd
            def _(eng):
                self.emit_engine("pool", eng)


class Ring:
    def __init__(self, items):
        self.items = items
        self.i = 0

    def get(self):
        it = self.items[self.i % len(self.items)]
        self.i += 1
        return it


PIECE_ELEMS = 4096


def piece_specs():
    P = []

    def ffn(pfx, gcol):
        for j2 in range(11):
            P.append((f"{pfx}_wi{j2}", 4096, [
                (f"{pfx}_w_in", 0, 8, 256 * j2, 256, 0, 256, gcol, 1),
                (f"{pfx}_w_in", 0, 8, DFF + 256 * j2, 256, 2048, 256, gcol, 1)]))
        for m in range(8):
            P.append((f"{pfx}_wo{m}", 2816, [
                (f"{pfx}_w_out", 0, 22, 128 * m, 128, 0, 128, None, 0)]))

    ffn("ffn1", C_FFN1)
    for g in range(2):
        for nm, base in (("hgi", 2048), ("hgf", 1024), ("hgq", 0), ("hgg", 3072)):
            P.append((f"{nm}{g}", 4096, [("w_in", 0, 8, base + 512 * g, 512, 0, 512, C_MIX, 1)]))
    P.append(("cq", 3584, [("w_in", 0, 8, 4096, 384, 0, 384, C_MIX, 1),
                           ("w_in", 0, 8, 4736, 64, 3072, 64, C_MIX, 1)]))
    P.append(("ckv", 2048, [("w_in", 0, 8, 4480, 256, 0, 256, C_MIX, 1)]))
    for g in range(2):
        P.append((f"qup{g}", 2304, [("w_q_up", 0, 3, 768 * g, 768, 0, 768, C_QL, 1)]))
    P.append(("kvk", 2048, [("w_kv_up", 0, 2, 256 * h, 128, 256 * h, 128, C_KVL, 1) for h in range(8)]))
    P.append(("kvv", 2048, [("w_kv_up", 0, 2, 256 * h + 128, 128, 128 * h, 1024, C_KVL, 1) for h in range(8)]))
    for m in range(8):
        P.append((f"mix{m}", 4096, [
            ("w_hg_branch", 0, 8, 128 * m, 128, 0, 128, C_HGO, 0),
            ("w_merge", 0, 8, 128 * m, 128, 1024, 128, C_MIX, 1),
            ("w_mla_branch", 0, 8, 128 * m, 128, 2048, 128, None, 0),
            ("w_merge", 0, 8, 1024 + 128 * m, 128, 3072, 128, C_MIX, 1)]))
    for g in range(2):
        P.append((f"wout{g}", 4096, [("w_out", 0, 8, 512 * g, 512, 0, 512, None, 0)]))
    ffn("ffn2", C_FFN2)
    return P


WSHAPES = {
    "ffn1_w_in": (D, 2 * DFF), "ffn1_w_out": (DFF, D), "w_in": (D, 4800),
    "w_hg_branch": (D, D), "w_q_up": (384, 1536), "w_kv_up": (256, 2048),
    "w_mla_branch": (D, D), "w_merge": (D, 2 * D), "w_out": (D, D),
    "ffn2_w_in": (D, 2 * DFF), "ffn2_w_out": (DFF, D),
}


def build(NT, debug=False, stages=9):
    S = NT * T
    nc = bass.Bass("TRN2", target_bir_lowering=False)
    xT_d = nc.dram_tensor("xT", [D, S], F32, kind="ExternalInput").ap()
    pos_d = nc.dram_tensor("pos", [1, S], I32, kind="ExternalInput").ap()
    consts_d = nc.dram_tensor("consts", [128, NCOL], F32, kind="ExternalInput").ap()
    mask_d = nc.dram_tensor("mask128", [128, 128], F32, kind="ExternalInput").ap()
    reset_d = nc.dram_tensor("resetm", [128, T], F32, kind="ExternalInput").ap()
    ident_d = nc.dram_tensor("ident", [128, 128], BF16, kind="ExternalInput").ap()
    ones_d = nc.dram_tensor("onesb", [128, 128], BF16, kind="ExternalInput").ap()
    W_d = {n: nc.dram_tensor(n, list(s), F32, kind="ExternalInput").ap() for n, s in WSHAPES.items()}
    outT_d = nc.dram_tensor("outT", [D, S], F32, kind="ExternalOutput").ap()
    specs = piece_specs()
    NP = len(specs)
    wp_d = nc.dram_tensor("wpieces", [NP, 128, PIECE_ELEMS], BF16, kind="Internal").ap()
    kvc_d = nc.dram_tensor("kvcache", [NT, 128, NH * 1536], BF16, kind="Internal").ap()
    dbg_d = {}

    with ExitStack() as semstack:
        tp = Tracker(nc, "p")
        with ExitStack() as st:
            cst = st.enter_context(nc.sbuf_tensor("p_consts", [128, NCOL], F32))
            NSTG = 2
            stg_f = [st.enter_context(nc.sbuf_tensor(f"p_sf{i}", [128, PIECE_ELEMS], F32)) for i in range(NSTG)]
            stg_b = [st.enter_context(nc.sbuf_tensor(f"p_sb{i}", [128, PIECE_ELEMS], BF16)) for i in range(NSTG)]
            b_cst = Buf("cst")
            b_sf = [Buf(f"sf{i}") for i in range(NSTG)]
            b_sb = [Buf(f"sb{i}") for i in range(NSTG)]
            b_wp = [Buf(f"wp{i}") for i in range(NP)]
            tp.op("sp", lambda e: e.dma_start(out=cst[:], in_=consts_d[:, :]), wr=[b_cst], dkey="c")
            for pi, (name, size, segs) in enumerate(specs):
                sl = pi % NSTG
                sf, sb = stg_f[sl], stg_b[sl]
                for (wn, r0, nk, c0, ncols, doff, kst, gcol, gstep) in segs:
                    src = W_d[wn][r0:r0 + nk * 128, c0:c0 + ncols].rearrange("(k p) c -> p k c", p=128)
                    dst = sf[:, doff:doff + nk * kst].rearrange("p (k c) -> p k c", c=kst)[:, :, 0:ncols]
                    tp.op("sp", (lambda e, dst=dst, src=src: e.dma_start(out=dst, in_=src)),
                          wr=[b_sf[sl]], dkey=f"l{sl}")
                use_act = (pi % 2 == 1)
                eng = "act" if use_act else "dve"
                anyg = any(s[7] is not None for s in segs)
                if not anyg:
                    if use_act:
                        fn = (lambda e, sb=sb, sf=sf, size=size:
                              e.activation(out=sb[:, 0:size], in_=sf[:, 0:size], func=AF.Copy))
                    else:
                        fn = (lambda e, sb=sb, sf=sf, size=size:
                              e.tensor_copy(out=sb[:, 0:size], in_=sf[:, 0:size]))
                    tp.op(eng, fn, rd=[b_sf[sl]], wr=[b_sb[sl]])
                else:
                    for (wn, r0, nk, c0, ncols, doff, kst, gcol, gstep) in segs:
                        for k in range(nk):
                            o_ = sb[:, doff + k * kst: doff + k * kst + ncols]
                            i_ = sf[:, doff + k * kst: doff + k * kst + ncols]
                            if gcol is None:
                                if use_act:
                                    fn = (lambda e, o_=o_, i_=i_: e.activation(out=o_, in_=i_, func=AF.Copy))
                                else:
                                    fn = (lambda e, o_=o_, i_=i_: e.tensor_copy(out=o_, in_=i_))
                            else:
                                g_ = cst[:, gcol + k * gstep: gcol + k * gstep + 1]
                                if use_act:
                                    fn = (lambda e, o_=o_, i_=i_, g_=g_:
                                          e.activation(out=o_, in_=i_, func=AF.Copy, scale=g_))
                                else:
                                    fn = (lambda e, o_=o_, i_=i_, g_=g_:
                                          e.tensor_scalar(out=o_, in0=i_, scalar1=g_, scalar2=None, op0=ALU.mult))
                            tp.op(eng, fn, rd=[b_sf[sl], b_cst], wr=[b_sb[sl]])
                tp.op("pool", (lambda e, pi=pi, sb=sb, size=size:
                               e.dma_start(out=wp_d[pi, :, 0:size], in_=sb[:, 0:size])),
                      rd=[b_sb[sl]], wr=[b_wp[pi]], dkey=f"s{sl}")
            tp.op("sp", None, rd=b_wp)
            tp.op("pool", None, rd=b_wp)
            tp.finalize()
            tp.alloc_sems(semstack)
            tp.run_block()

        tk = Tracker(nc, "m")
        with ExitStack() as st:
            def sb(name, shape, dt):
                return st.enter_context(nc.sbuf_tensor(name, shape, dt))

            CST = sb("CST", [128, NCOL], F32)
            MISC = sb("MISC", [128, 32], F32)
            MASK = sb("MASK", [128, 128], F32)
            RESET = sb("RESET", [128, T], F32)
            IDENT = sb("IDENT", [128, 128], BF16)
            ONES = sb("ONES", [128, 128], BF16)
            XT = sb("XT", [128, 8, T], F32)
            XN = sb("XN", [128, 8, T], BF16)
            AR1 = sb("AR1", [128, 24 * 512], BF16)
            HID = AR1[:, 0:22 * 512].rearrange("p (c t) -> p c t", t=512)
            CU = AR1[:, :].rearrange("p (h e) -> p h e", e=1536)
            NW = 3
            WR = [sb(f"WR{i}", [128, PIECE_ELEMS], BF16) for i in range(NW)]
            NTMP = 8
            TMP = [sb(f"TMP{i}", [128, T], F32) for i in range(NTMP)]
            NSQ = 4
            SQ = [sb(f"SQ{i}", [128, T], BF16) for i in range(NSQ)]
            QT = sb("QT", [128, 4, T], BF16)
            KT = sb("KT", [128, 4, T], BF16)
            EC4 = sb("EC4", [128, 4, T], F32)
            VTOK = sb("VTOK", [128, 4, 512], BF16)
            G = sb("G", [128, 4, T], BF16)
            SS_ = sb("S", [128, NH, 128], F32)
            SB16 = sb("SB16", [128, NH, 128], BF16)
            EL = sb("EL", [128, NH, 8], F32)
            KTOK = [sb(f"KTOK{i}", [128, 128], BF16) for i in range(4)]
            ATM = [sb(f"ATM{i}", [128, 128], BF16) for i in range(4)]
            YH = sb("YH", [128, 8, T], BF16)
            ATT = sb("ATT", [128, 8, T], BF16)
            QM = sb("QM", [128, 8, T], BF16)
            QRT = sb("QRT", [128, 8, T], BF16)
            CQN = sb("CQN", [128, 3, T], BF16)
            CKVN = sb("CKVN", [128, 2, T], BF16)
            KPR = sb("KPR", [128, T], F32)
            SQPE = sb("SQPE", [128, T], BF16)
            CC = sb("CC", [128, T], F32)
            NS = sb("NS", [128, T], F32)
            POSI = sb("POSI", [128, T], I32)
            NC_ = 4
            CR = [sb(f"CR{i}", [128, 1536], BF16) for i in range(NC_)]
            NPT = 4
            PT = [sb(f"PT{i}", [128, T], BF16) for i in range(NPT)]
            PS = [st.enter_context(nc.psum_tensor(f"PS{i}", [128, 512], F32)) for i in range(6)]
            PSQ = st.enter_context(nc.psum_tensor("PSQ", [128, 512], F32))
            PSB = st.enter_context(nc.psum_tensor("PSB", [128, 1024], BF16))

            b_CST, b_MISC, b_MASK, b_RESET, b_ID, b_ONES = (Buf(n) for n in
                                                            ("CST", "MISC", "MASK", "RESET", "ID", "ONES"))
            b_XT = [Buf(f"XT{c}") for c in range(8)]
            b_XN = [Buf(f"XN{c}") for c in range(8)]
            b_AR = [Buf(f"AR{c}") for c in range(24)]
            b_HID = b_AR[:22]

            def b_CU(h):
                return b_AR[3 * h:3 * h + 3]

            wring = Ring([(WR[i], Buf(f"WR{i}")) for i in range(NW)])
            tmpr = Ring([(TMP[i], Buf(f"TMP{i}")) for i in range(NTMP)])
            sqr = Ring([(SQ[i], Buf(f"SQ{i}")) for i in range(NSQ)])
            psr = Ring([(PS[i], Buf(f"PS{i}", excl=True)) for i in range(6)])
            b_PS = [psr.items[i][1] for i in range(6)]
            psq = Ring([(PS[4][:, 0:128], b_PS[4]), (PS[5][:, 0:128], b_PS[5]),
                        (PSQ[:, 0:128], Buf("PSQ", excl=True))])
            psb = Ring([(PSB[:, 0:128], Buf("PSB", excl=True))])
            ktokr = Ring([(KTOK[i], Buf(f"KTOK{i}")) for i in range(4)])
            atmr = Ring([(ATM[i], Buf(f"ATM{i}")) for i in range(4)])
            cring = Ring([(CR[i], Buf(f"CR{i}")) for i in range(NC_)])
            ptr = Ring([(PT[i], Buf(f"PT{i}")) for i in range(NPT)])
            b_QT = [Buf(f"QT{h}") for h in range(4)]
            b_KT = [Buf(f"KT{h}") for h in range(4)]
            b_EC = [Buf(f"EC{h}") for h in range(4)]
            b_VT = [Buf(f"VT{b}") for b in range(4)]
            b_G = [Buf(f"G{h}") for h in range(4)]
            b_S = [Buf(f"S{h}") for h in range(NH)]
            b_S16 = [Buf(f"S16{h}") for h in range(NH)]
            b_EL = [Buf(f"EL{h}") for h in range(NH)]
            b_YH = [Buf(f"YH{h}") for h in range(8)]
            b_ATT = [Buf(f"ATT{h}") for h in range(8)]
            b_QM = [Buf(f"QM{h}") for h in range(8)]
            b_QRT = [Buf(f"QRT{h}") for h in range(8)]
            b_CQN = [Buf(f"CQN{c}") for c in range(3)]
            b_CKVN = [Buf(f"CKVN{c}") for c in range(2)]
            b_KPR, b_SQPE, b_CC, b_NS, b_POSI = (Buf(n) for n in ("KPR", "SQPE", "CC", "NS", "POSI"))
            b_KVC = [Buf(f"KVC{i}") for i in range(NT)]
            b_OUT = [Buf(f"OUT{i}") for i in range(NT)]

            def mm(out, pairs, rd, wr):
                def fn(e):
                    n = len(pairs)
                    inst = None
                    for q, (l, r) in enumerate(pairs):
                        inst = e.matmul(out, lhsT=l, rhs=r, start=(q == 0), stop=(q == n - 1))
                    return inst
                return tk.op("pe", fn, rd=rd, wr=wr)

            def mm1(out, l, r, start, stop, rd, wr):
                return tk.op("pe", lambda e: e.matmul(out, lhsT=l, rhs=r, start=start, stop=stop),
                             rd=rd + wr if not start else rd, wr=wr)

            def act(out, in_, func, rd, wr, scale=None, bias=None):
                kw = {}
                if scale is not None:
                    kw["scale"] = scale
                if bias is not None:
                    kw["bias"] = bias
                return tk.op("act", lambda e: e.activation(out=out, in_=in_, func=func, **kw), rd=rd, wr=wr)

            def tt(out, in0, in1, op, rd, wr):
                return tk.op("dve", lambda e: e.tensor_tensor(out=out, in0=in0, in1=in1, op=op), rd=rd, wr=wr)

            def ptt(out, in0, in1, op, rd, wr):
                return tk.op("pool", lambda e: e.tensor_tensor(out=out, in0=in0, in1=in1, op=op), rd=rd, wr=wr)

            def ascale(out, in_, scale_ap, rd, wr):
                return tk.op("act", lambda e: e.activation(out=out, in_=in_, func=AF.Copy, scale=scale_ap),
                             rd=rd, wr=wr)

            def ts(out, in0, s1, op0, rd, wr, s2=None, op1=None):
                if op1 is None:
                    return tk.op("dve", lambda e: e.tensor_scalar(out=out, in0=in0, scalar1=s1, scalar2=None,
                                                                  op0=op0), rd=rd, wr=wr)
                return tk.op("dve", lambda e: e.tensor_scalar(out=out, in0=in0, scalar1=s1, scalar2=s2,
                                                              op0=op0, op1=op1), rd=rd, wr=wr)

            def stt(out, in0, scalar, in1, op0, op1, rd, wr):
                return tk.op("dve", lambda e: e.scalar_tensor_tensor(out=out, in0=in0, scalar=scalar, in1=in1,
                                                                     op0=op0, op1=op1), rd=rd, wr=wr)

            def recip(out, in_, rd, wr):
                return tk.op("dve", lambda e: e.reciprocal(out=out, in_=in_), rd=rd, wr=wr)

            def dcopy(out, in_, rd, wr):
                return tk.op("dve", lambda e: e.tensor_copy(out=out, in_=in_), rd=rd, wr=wr)

            def acopy(out, in_, rd, wr):
                return tk.op("act", lambda e: e.activation(out=out, in_=in_, func=AF.Copy), rd=rd, wr=wr)

            def dbg(name, ap, bufs, shape, dt):
                if not debug:
                    return
                d = nc.dram_tensor("dbg_" + name, list(shape), dt, kind="ExternalOutput").ap()
                b = Buf("dbg_" + name)
                dbg_d[name] = b
                tk.op("pool", lambda e: e.dma_start(out=d, in_=ap), rd=bufs, wr=[b], dkey="dbg_" + name)

            EPSC = MISC[:, 0:1]
            ONEC = MISC[:, 1:2]
            OML = MISC[:, 2:10]
            LBD = MISC[:, 10:18]

            tk.op("sp", lambda e: e.dma_start(out=CST[:], in_=consts_d[:, :]), wr=[b_CST], dkey="i0")
            tk.op("sp", lambda e: e.dma_start(out=MASK[:], in_=mask_d[:, :]), wr=[b_MASK], dkey="i1")
            tk.op("sp", lambda e: e.dma_start(out=RESET[:], in_=reset_d[:, :]), wr=[b_RESET], dkey="i2")
            tk.op("sp", lambda e: e.dma_start(out=IDENT[:], in_=ident_d[:, :]), wr=[b_ID], dkey="i3")
            tk.op("sp", lambda e: e.dma_start(out=ONES[:], in_=ones_d[:, :]), wr=[b_ONES], dkey="i4")
            tk.op("dve", lambda e: e.memset(MISC[:, 0:1], EPS), wr=[b_MISC])
            tk.op("dve", lambda e: e.memset(MISC[:, 1:2], 1.0), wr=[b_MISC])
            tk.op("dve", lambda e: e.memset(SS_[:], 0.0), wr=b_S)
            tk.op("dve", lambda e: e.memset(SB16[:], 0.0), wr=b_S16)
            tt(LBD, CST[:, C_LB1:C_LB1 + 8], CST[:, C_LB0:C_LB0 + 8], ALU.subtract, rd=[b_CST, b_MISC], wr=[b_MISC])
            act(OML, LBD, AF.Sigmoid, rd=[b_MISC], wr=[b_MISC])

            pidx = {s[0]: k for k, s in enumerate(specs)}
            seq = [s[0] for s in specs]
            wstate = {"issued": 0, "cur": 0, "slots": {}}
            total_pieces = NT * len(seq)
            PF = NW - 1

            def w_issue(n):
                name = seq[n % len(seq)]
                pi = pidx[name]
                size = specs[pi][1]
                ap, b = wring.items[n % NW]
                tk.op("sp", lambda e: e.dma_start(out=ap[:, 0:size], in_=wp_d[pi, :, 0:size]),
                      wr=[b], dkey=f"w{n % NW}")

            def wget(expect):
                if stages < 9:
                    while seq[wstate["cur"] % len(seq)] != expect:
                        wstate["cur"] += 1
                        wstate["issued"] = max(wstate["issued"], wstate["cur"])
                n = wstate["cur"]
                assert seq[n % len(seq)] == expect, (seq[n % len(seq)], expect)
                while wstate["issued"] < min(n + PF, total_pieces):
                    w_issue(wstate["issued"])
                    wstate["issued"] += 1
                wstate["cur"] += 1
                return wring.items[n % NW]

            cseq = [(i, h, j) for i in range(NT) for h in range(NH) for j in range(i)]
            cstate = {"issued": 0, "cur": 0}
            CPF = NC_ - 2

            def c_issue(n):
                i, h, j = cseq[n]
                ap, b = cring.items[n % NC_]
                tk.op("sp", lambda e: e.dma_start(out=ap[:, :], in_=kvc_d[j, :, 1536 * h:1536 * h + 1536]),
                      rd=[b_KVC[j]], wr=[b], dkey=f"c{n % NC_}")

            def cget(i, h, j):
                n = cstate["cur"]
                assert cseq[n] == (i, h, j)
                while cstate["issued"] < min(n + CPF, len(cseq)) and (
                        cstate["issued"] <= n or cseq[cstate["issued"]][0] <= i):
                    c_issue(cstate["issued"])
                    cstate["issued"] += 1
                cstate["cur"] += 1
                return cring.items[n % NC_]

            def _rms_rb_unused(srcs, nfeat):
                pa, pb = psr.get()
                n = len(srcs)
                for q, (ap, bufs, npart) in enumerate(srcs):
                    sa, sbf = sqr.get()
                    act(sa[0:npart, :], ap, AF.Square, rd=bufs, wr=[sbf])
                    mm1(pa[:, :], ONES[0:npart, :], sa[0:npart, :], q == 0, q == n - 1,
                        rd=[b_ONES, sbf], wr=[pb])
                ra, rbf = tmpr.get()
                act(ra[:, :], pa[:, :], AF.Ln, rd=[pb, b_MISC], wr=[rbf], scale=1.0 / nfeat, bias=EPSC)
                r2, r2b = tmpr.get()
                act(r2[:, :], ra[:, :], AF.Exp, rd=[rbf], wr=[r2b], scale=-0.5)
                return r2, r2b

            def norm_to_xn():
                rb, rbb = rms_rb([(XT[:, c, :], [b_XT[c]], 128) for c in range(8)], D)
                for c in range(8):
                    tt(XN[:, c, :], XT[:, c, :], rb[:, :], ALU.mult, rd=[b_XT[c], rbb], wr=[b_XN[c]])

            def ffn(pfx):
                norm_to_xn()
                for j2 in range(11):
                    wa, wb = wget(f"{pfx}_wi{j2}")
                    for jj in range(2):
                        j = 2 * j2 + jj
                        ga, gb = psr.get()
                        ua, ub = psr.get()
                        mm(ga[:, :], [(wa[:, k * 256 + jj * 128:k * 256 + jj * 128 + 128], XN[:, k, :])
                                      for k in range(8)], rd=[wb] + b_XN, wr=[gb])
                        mm(ua[:, :], [(wa[:, 2048 + k * 256 + jj * 128:2048 + k * 256 + jj * 128 + 128], XN[:, k, :])
                                      for k in range(8)], rd=[wb] + b_XN, wr=[ub])
                        sa, sbf = tmpr.get()
                        act(sa[:, :], ga[:, :], AF.Silu, rd=[gb], wr=[sbf])
                        tt(HID[:, j, :], sa[:, :], ua[:, :], ALU.mult, rd=[sbf, ub], wr=[b_HID[j]])
                for m in range(8):
                    wa, wb = wget(f"{pfx}_wo{m}")
                    pa, pb = psr.get()
                    mm(pa[:, :], [(wa[:, k * 128:k * 128 + 128], HID[:, k, :]) for k in range(22)],
                       rd=[wb] + b_HID, wr=[pb])
                    stt(XT[:, m, :], pa[:, :], 0.5, XT[:, m, :], ALU.mult, ALU.add,
                        rd=[pb, b_XT[m]], wr=[b_XT[m]])

            def rope_tables(i):
                c0 = i * T
                tk.op("sp", lambda e: e.dma_start(out=POSI[0:64, :],
                                                  in_=pos_d[0:1, c0:c0 + T].partition_broadcast(64)),
                      wr=[b_POSI], dkey="pos")
                pf, pfb = tmpr.get()
                dcopy(pf[0:64, :], POSI[0:64, :], rd=[b_POSI], wr=[pfb])
                ang, angb = tmpr.get()
                ts(ang[0:64, :], pf[0:64, :], CST[0:64, C_INVF:C_INVF + 1], ALU.mult, rd=[pfb, b_CST], wr=[angb])
                r, rb_ = tmpr.get()
                ts(r[0:64, :], ang[0:64, :], MAGIC, ALU.add, rd=[angb], wr=[rb_], s2=MAGIC, op1=ALU.subtract)
                fr, frb = tmpr.get()
                tt(fr[0:64, :], ang[0:64, :], r[0:64, :], ALU.subtract, rd=[angb, rb_], wr=[frb])
                act(NS[0:64, :], fr[0:64, :], AF.Sin, rd=[frb, b_CST], wr=[b_NS],
                    scale=CST[0:64, C_S2PI:C_S2PI + 1])
                a2, a2b = tmpr.get()
                ts(a2[0:64, :], ang[0:64, :], 0.25, ALU.add, rd=[angb], wr=[a2b])
                r2, r2b = tmpr.get()
                ts(r2[0:64, :], a2[0:64, :], MAGIC, ALU.add, rd=[a2b], wr=[r2b], s2=MAGIC, op1=ALU.subtract)
                f2, f2b = tmpr.get()
                tt(f2[0:64, :], a2[0:64, :], r2[0:64, :], ALU.subtract, rd=[a2b, r2b], wr=[f2b])
                act(CC[0:64, :], f2[0:64, :], AF.Sin, rd=[f2b], wr=[b_CC], scale=TWO_PI)

            def rope(src, srcb, dst, dstb):
                t2, t2b = tmpr.get()
                ptt(dst[0:64, :], src[0:64, :], CC[0:64, :], ALU.mult, rd=[srcb, b_CC], wr=[dstb])
                tt(t2[0:32, :], src[32:64, :], NS[32:64, :], ALU.mult, rd=[srcb, b_NS], wr=[t2b])
                tt(t2[32:64, :], src[0:32, :], NS[0:32, :], ALU.mult, rd=[srcb, b_NS, t2b], wr=[t2b])
                ptt(dst[0:64, :], dst[0:64, :], t2[0:64, :], ALU.add, rd=[dstb, t2b], wr=[dstb])

            def hgrn_group(g):
                wa, wb = wget(f"hgi{g}")
                for tb in range(4):
                    pa, pb = psr.get()
                    mm(pa[:, :], [(XN[:, k, 128 * tb:128 * tb + 128], wa[:, k * 512:k * 512 + 512])
                                  for k in range(8)], rd=[wb] + b_XN, wr=[pb])
                    acopy(VTOK[:, tb, :], pa[:, :], rd=[pb], wr=[b_VT[tb]])
                wa, wb = wget(f"hgf{g}")
                for hh in range(4):
                    h = 4 * g + hh
                    za, zb = psr.get()
                    mm(za[:, :], [(wa[:, k * 512 + hh * 128:k * 512 + hh * 128 + 128], XN[:, k, :])
                                  for k in range(8)], rd=[wb] + b_XN, wr=[zb])
                    sg, sgb = tmpr.get()
                    act(sg[:, :], za[:, :], AF.Sigmoid, rd=[zb], wr=[sgb], scale=-1.0)
                    kk, kkb = tmpr.get()
                    ts(kk[:, :], sg[:, :], OML[:, h:h + 1], ALU.mult, rd=[sgb, b_MISC], wr=[kkb])
                    lf, lfb = tmpr.get()
                    act(lf[:, :], kk[:, :], AF.Ln, rd=[kkb, b_MISC], wr=[lfb], scale=-1.0, bias=ONEC)
                    cu, cub = tmpr.get()
                    tk.op("dve", lambda e, cu=cu, lf=lf: e.tensor_tensor_scan(
                        out=cu[:, :], data0=RESET[:, :], data1=lf[:, :], initial=0.0,
                        op0=ALU.mult, op1=ALU.add), rd=[lfb, b_RESET], wr=[cub])
                    ts(cu[:, :], cu[:, :], -80.0, ALU.max, rd=[cub], wr=[cub])
                    act(EC4[:, hh, :], cu[:, :], AF.Exp, rd=[cub], wr=[b_EC[hh]])
                    dcopy(EL[:, h, :], EC4[:, hh, 63:T:64], rd=[b_EC[hh]], wr=[b_EL[h]])
                    en, enb = tmpr.get()
                    act(en[:, :], cu[:, :], AF.Exp, rd=[cub], wr=[enb], scale=-1.0)
                    tt(KT[:, hh, :], kk[:, :], en[:, :], ALU.mult, rd=[kkb, enb], wr=[b_KT[hh]])
                wa, wb = wget(f"hgq{g}")
                for hh in range(4):
                    qa, qb = psr.get()
                    mm(qa[:, :], [(wa[:, k * 512 + hh * 128:k * 512 + hh * 128 + 128], XN[:, k, :])
                                  for k in range(8)], rd=[wb] + b_XN, wr=[qb])
                    sq_, sqb_ = tmpr.get()
                    act(sq_[:, :], qa[:, :], AF.Silu, rd=[qb], wr=[sqb_])
                    tt(QT[:, hh, :], sq_[:, :], EC4[:, hh, :], ALU.mult, rd=[sqb_, b_EC[hh]], wr=[b_QT[hh]])
                wa, wb = wget(f"hgg{g}")
                for hh in range(4):
                    qa, qb = psr.get()
                    mm(qa[:, :], [(wa[:, k * 512 + hh * 128:k * 512 + hh * 128 + 128], XN[:, k, :])
                                  for k in range(8)], rd=[wb] + b_XN, wr=[qb])
                    act(G[:, hh, :], qa[:, :], AF.Silu, rd=[qb], wr=[b_G[hh]])
                for b in range(4):
                    cs = slice(128 * b, 128 * b + 128)
                    atm_l, ktok_l = [], []
                    for hh in range(4):
                        aa, ab = psq.get()
                        mm(aa, [(KT[:, hh, cs], QT[:, hh, cs])], rd=[b_KT[hh], b_QT[hh]], wr=[ab])
                        ma, mb = atmr.get()
                        tt(ma[:, :], aa, MASK[:, :], ALU.mult, rd=[ab, b_MASK], wr=[mb])
                        ta, tb_ = psb.get()
                        tk.op("pe", lambda e, ta=ta, hh=hh, cs=cs: e.transpose(ta, KT[:, hh, cs], IDENT[:, :]),
                              rd=[b_KT[hh], b_ID], wr=[tb_])
                        ka, kb = ktokr.get()
                        acopy(ka[:, :], ta, rd=[tb_], wr=[kb])
                        atm_l.append((ma, mb))
                        ktok_l.append((ka, kb))
                    for cc in range(2):
                        c = 2 * b + cc
                        c64 = slice(128 * b + 64 * cc, 128 * b + 64 * cc + 64)
                        for hh in range(4):
                            h = 4 * g + hh
                            if cc == 0:
                                mm1(PS[hh][:, cs], VTOK[:, b, hh * 128:hh * 128 + 128], atm_l[hh][0][:, :],
                                    True, False, rd=[b_VT[b], atm_l[hh][1]], wr=[b_PS[hh]])
                            mm1(PS[hh][:, c64], SB16[:, h, :], QT[:, hh, c64], False, cc == 1,
                                rd=[b_S16[h], b_QT[hh]], wr=[b_PS[hh]])
                            sa, sbf = psq.get()
                            ka, kb = ktok_l[hh]
                            mm(sa, [(ka[64 * cc:64 * cc + 64, :],
                                     VTOK[64 * cc:64 * cc + 64, b, hh * 128:hh * 128 + 128])],
                               rd=[kb, b_VT[b]], wr=[sbf])
                            el = EL[:, h, c:c + 1]
                            ts(SS_[:, h, :], SS_[:, h, :], el, ALU.mult, rd=[b_S[h], b_EL[h]], wr=[b_S[h]])
                            stt(SS_[:, h, :], sa, el, SS_[:, h, :], ALU.mult, ALU.add,
                                rd=[sbf, b_EL[h], b_S[h]], wr=[b_S[h]])
                            acopy(SB16[:, h, :], SS_[:, h, :], rd=[b_S[h]], wr=[b_S16[h]])
                for hh in range(4):
                    h = 4 * g + hh
                    sa, sbf = sqr.get()
                    act(sa[:, :], PS[hh][:, :], AF.Square, rd=[b_PS[hh]], wr=[sbf])
                    pa, pb = PS[4 + hh % 2], b_PS[4 + hh % 2]
                    mm(pa[:, :], [(ONES[:, :], sa[:, :])], rd=[b_ONES, sbf], wr=[pb])
                    ra, rbf = tmpr.get()
                    act(ra[:, :], pa[:, :], AF.Ln, rd=[pb, b_MISC], wr=[rbf], scale=1.0 / 128, bias=EPSC)
                    r2, r2b = tmpr.get()
                    act(r2[:, :], ra[:, :], AF.Exp, rd=[rbf], wr=[r2b], scale=-0.5)
                    y, yb = tmpr.get()
                    tt(y[:, :], PS[hh][:, :], r2[:, :], ALU.mult, rd=[b_PS[hh], r2b], wr=[yb])
                    tt(YH[:, h, :], y[:, :], G[:, hh, :], ALU.mult, rd=[yb, b_G[hh]], wr=[b_YH[h]])

            def mla_proj(i):
                rope_tables(i)
                wa, wb = wget("cq")
                pq = [psr.get() for _ in range(3)]
                for c in range(3):
                    mm(pq[c][0][:, :], [(wa[:, k * 384 + c * 128:k * 384 + c * 128 + 128], XN[:, k, :])
                                        for k in range(8)], rd=[wb] + b_XN, wr=[pq[c][1]])
                pk, pkb = psr.get()
                mm(pk[0:64, :], [(wa[:, 3072 + k * 64:3072 + k * 64 + 64], XN[:, k, :]) for k in range(8)],
                   rd=[wb] + b_XN, wr=[pkb])
                rb, rbb = rms_rb([(pq[c][0][:, :], [pq[c][1]], 128) for c in range(3)], 384)
                for c in range(3):
                    tt(CQN[:, c, :], pq[c][0][:, :], rb[:, :], ALU.mult, rd=[pq[c][1], rbb], wr=[b_CQN[c]])
                act(SQPE[0:64, :], pk[0:64, :], AF.Square, rd=[pkb], wr=[b_SQPE])
                kg, kgb = tmpr.get()
                ascale(kg[0:64, :], pk[0:64, :], CST[0:64, C_GKR:C_GKR + 1], rd=[pkb, b_CST], wr=[kgb])
                rope(kg, kgb, KPR, b_KPR)
                wa, wb = wget("ckv")
                pc = [psr.get() for _ in range(2)]
                for c in range(2):
                    mm(pc[c][0][:, :], [(wa[:, k * 256 + c * 128:k * 256 + c * 128 + 128], XN[:, k, :])
                                        for k in range(8)], rd=[wb] + b_XN, wr=[pc[c][1]])
                rb, rbb = rms_rb([(pc[c][0][:, :], [pc[c][1]], 128) for c in range(2)], 256)
                for c in range(2):
                    tt(CKVN[:, c, :], pc[c][0][:, :], rb[:, :], ALU.mult, rd=[pc[c][1], rbb], wr=[b_CKVN[c]])
                for h in range(NH):
                    hh = h % 4
                    if hh == 0:
                        wa, wb = wget(f"qup{h // 4}")
                    qn, qnb = psr.get()
                    qr, qrb = psr.get()
                    o0 = hh * 192
                    mm(qn[:, :], [(wa[:, k * 768 + o0:k * 768 + o0 + 128], CQN[:, k, :]) for k in range(3)],
                       rd=[wb] + b_CQN, wr=[qnb])
                    mm(qr[0:64, :], [(wa[:, k * 768 + o0 + 128:k * 768 + o0 + 192], CQN[:, k, :]) for k in range(3)],
                       rd=[wb] + b_CQN, wr=[qrb])
                    rq, rqb = rms_rb([(qn[:, :], [qnb], 128), (qr[0:64, :], [qrb], 64)], 192)
                    stt(QM[:, h, :], qn[:, :], CST[:, C_GQN:C_GQN + 1], rq[:, :], ALU.mult, ALU.mult,
                        rd=[qnb, b_CST, rqb], wr=[b_QM[h]])
                    qg, qgb = tmpr.get()
                    ascale(qg[0:64, :], qr[0:64, :], CST[0:64, C_GQR:C_GQR + 1], rd=[qrb, b_CST], wr=[qgb])
                    t1, t1b = tmpr.get()
                    rope(qg, qgb, t1, t1b)
                    ptt(QRT[0:64, h, :], t1[0:64, :], rq[0:64, :], ALU.mult, rd=[t1b, rqb], wr=[b_QRT[h]])
                wa, wb = wget("kvk")
                for h in range(NH):
                    kn, knb = psr.get()
                    mm(kn[:, :], [(wa[:, h * 256 + k * 128:h * 256 + k * 128 + 128], CKVN[:, k, :]) for k in range(2)],
                       rd=[wb] + b_CKVN, wr=[knb])
                    rk, rkb = rms_rb([(kn[:, :], [knb], 128), (None, None, 64)], 192)
                    stt(CU[:, h, 0:512], kn[:, :], CST[:, C_GKN:C_GKN + 1], rk[:, :], ALU.mult, ALU.mult,
                        rd=[knb, b_CST, rkb], wr=b_CU(h))
                    ptt(CU[0:64, h, 512:1024], KPR[0:64, :], rk[0:64, :], ALU.mult,
                        rd=[b_KPR, rkb] + b_CU(h), wr=b_CU(h))
                wa, wb = wget("kvv")
                for tb in range(4):
                    for half in range(2):
                        pv, pvb = psr.get()
                        mm(pv[:, :], [(CKVN[:, k, 128 * tb:128 * tb + 128],
                                       wa[:, k * 1024 + half * 512:k * 1024 + half * 512 + 512]) for k in range(2)],
                           rd=[wb] + b_CKVN, wr=[pvb])
                        dst = CU[:, 4 * half:4 * half + 4, 1024 + 128 * tb:1024 + 128 * tb + 128]
                        src = pv[:, :].rearrange("p (h d) -> p h d", d=128)
                        cb = []
                        for h in range(4 * half, 4 * half + 4):
                            cb += b_CU(h)
                        acopy(dst, src, rd=[pvb] + cb, wr=cb)
                tk.op("pool", lambda e: e.dma_start(out=kvc_d[i, :, :], in_=AR1[:, :]),
                      rd=b_AR, wr=[b_KVC[i]], dkey="cs")


            def rms_rb(srcs, nfeat):
                pa, pb = psr.get()
                n = len(srcs)
                for q, (ap, bufs, npart) in enumerate(srcs):
                    if ap is None:
                        mm1(pa[:, :], ONES[0:npart, :], SQPE[0:npart, :], q == 0, q == n - 1,
                            rd=[b_ONES, b_SQPE], wr=[pb])
                        continue
                    sa, sbf = sqr.get()
                    act(sa[0:npart, :], ap, AF.Square, rd=bufs, wr=[sbf])
                    mm1(pa[:, :], ONES[0:npart, :], sa[0:npart, :], q == 0, q == n - 1,
                        rd=[b_ONES, sbf], wr=[pb])
                ra, rbf = tmpr.get()
                act(ra[:, :], pa[:, :], AF.Ln, rd=[pb, b_MISC], wr=[rbf], scale=1.0 / nfeat, bias=EPSC)
                r2, r2b = tmpr.get()
                act(r2[:, :], ra[:, :], AF.Exp, rd=[rbf], wr=[r2b], scale=-0.5)
                return r2, r2b

            def attention(i):
                sidx = [0]
                for h in range(NH):
                    Ob, Obb = PS[2 + h % 2], b_PS[2 + h % 2]
                    Db, Dbb = PS[4 + h % 2], b_PS[4 + h % 2]
                    nblk = 4 * i + 4

                    def gen_blocks():
                        for j in range(i):
                            ca, cb = cget(i, h, j)
                            for kb in range(4):
                                yield (ca[:, 128 * kb:128 * kb + 128],
                                       ca[0:64, 512 + 128 * kb:512 + 128 * kb + 128],
                                       ca[:, 1024 + 128 * kb:1024 + 128 * kb + 128], 0, False, [cb])
                        for kb in range(4):
                            yield (CU[:, h, 128 * kb:128 * kb + 128],
                                   CU[0:64, h, 512 + 128 * kb:512 + 128 * kb + 128],
                                   CU[:, h, 1024 + 128 * kb:1024 + 128 * kb + 128], 128 * kb, True, b_CU(h))

                    def emit_s(blk):
                        kn, kr, v, q0, diag, kbufs = blk
                        Sb, Sbb = PS[sidx[0]], b_PS[sidx[0]]
                        sidx[0] ^= 1
                        mm(Sb[:, q0:T], [(kn, QM[:, h, q0:T]), (kr, QRT[0:64, h, q0:T])],
                           rd=kbufs + [b_QM[h], b_QRT[h]], wr=[Sbb])
                        pt, ptb = ptr.get()
                        act(pt[:, q0:T], Sb[:, q0:T], AF.Exp, rd=[Sbb], wr=[ptb], scale=SCALE)
                        if diag:
                            tk.op("dve", lambda e: e.memset(pt[64:128, q0:q0 + 64], 0.0), rd=[ptb], wr=[ptb])
                        return (pt, ptb)

                    def emit_od(blk, p, n):
                        kn, kr, v, q0, diag, kbufs = blk
                        pt, ptb = p
                        first = n == 0
                        last = n == nblk - 1
                        mm1(Ob[:, q0:T], v, pt[:, q0:T], first, last, rd=kbufs + [ptb], wr=[Obb])
                        mm1(Db[:, q0:T], ONES[:, :], pt[:, q0:T], first, last, rd=[b_ONES, ptb], wr=[Dbb])

                    prev = None
                    n = 0
                    for blk in gen_blocks():
                        cur = (blk, emit_s(blk))
                        if prev is not None:
                            emit_od(prev[0], prev[1], n)
                            n += 1
                        prev = cur
                    emit_od(prev[0], prev[1], n)
                    rl_, rlb = tmpr.get()
                    act(rl_[:, :], Db[:, :], AF.Ln, rd=[Dbb], wr=[rlb])
                    rd_, rdb = tmpr.get()
                    act(rd_[:, :], rl_[:, :], AF.Exp, rd=[rlb], wr=[rdb], scale=-1.0)
                    tt(ATT[:, h, :], Ob[:, :], rd_[:, :], ALU.mult, rd=[Obb, rdb], wr=[b_ATT[h]])

            def merge_out():
                for m in range(8):
                    wa, wb = wget(f"mix{m}")
                    A, Ab = psr.get()
                    B, Bb = psr.get()
                    C, Cb = psr.get()
                    Dp, Dpb = psr.get()
                    mm(A[:, :], [(wa[:, k * 128:k * 128 + 128], YH[:, k, :]) for k in range(8)],
                       rd=[wb] + b_YH, wr=[Ab])
                    mm(B[:, :], [(wa[:, 1024 + k * 128:1024 + k * 128 + 128], XN[:, k, :]) for k in range(8)],
                       rd=[wb] + b_XN, wr=[Bb])
                    mm(C[:, :], [(wa[:, 2048 + k * 128:2048 + k * 128 + 128], ATT[:, k, :]) for k in range(8)],
                       rd=[wb] + b_ATT, wr=[Cb])
                    mm(Dp[:, :], [(wa[:, 3072 + k * 128:3072 + k * 128 + 128], XN[:, k, :]) for k in range(8)],
                       rd=[wb] + b_XN, wr=[Dpb])
                    g1, g1b = tmpr.get()
                    act(g1[:, :], B[:, :], AF.Sigmoid, rd=[Bb, b_CST], wr=[g1b], bias=CST[:, C_BM + m:C_BM + m + 1])
                    g2, g2b = tmpr.get()
                    act(g2[:, :], Dp[:, :], AF.Sigmoid, rd=[Dpb, b_CST], wr=[g2b],
                        bias=CST[:, C_BM + 8 + m:C_BM + 8 + m + 1])
                    t1, t1b = tmpr.get()
                    tt(t1[:, :], A[:, :], g1[:, :], ALU.mult, rd=[Ab, g1b], wr=[t1b])
                    t2, t2b = tmpr.get()
                    tt(t2[:, :], C[:, :], g2[:, :], ALU.mult, rd=[Cb, g2b], wr=[t2b])
                    tt(QM[:, m, :], t1[:, :], t2[:, :], ALU.add, rd=[t1b, t2b], wr=[b_QM[m]])
                for m in range(8):
                    if m % 4 == 0:
                        wa, wb = wget(f"wout{m // 4}")
                    pa, pb = psr.get()
                    mm(pa[:, :], [(wa[:, k * 512 + (m % 4) * 128:k * 512 + (m % 4) * 128 + 128], QM[:, k, :])
                                  for k in range(8)], rd=[wb] + b_QM, wr=[pb])
                    tt(XT[:, m, :], pa[:, :], XT[:, m, :], ALU.add, rd=[pb, b_XT[m]], wr=[b_XT[m]])

            for i in range(NT):
                c0 = i * T
                tk.op("sp", lambda e, c0=c0: e.dma_start(
                    out=XT[:, :, :], in_=xT_d[:, c0:c0 + T].rearrange("(c p) t -> p c t", p=128)),
                    wr=b_XT, dkey="x")
                if stages >= 1:
                    ffn("ffn1")
                if debug and i == 0:
                    dbg("h1", XT[:, :, :], b_XT, [128, 8, T], F32)
                if stages >= 2:
                    norm_to_xn()
                    hgrn_group(0)
                    hgrn_group(1)
                if debug and i == NT - 1:
                    dbg("yh", YH[:, :, :], b_YH, [128, 8, T], BF16)
                if stages >= 3:
                    mla_proj(i)
                if debug and i == NT - 1:
                    dbg("qm", QM[:, :, :], b_QM, [128, 8, T], BF16)
                    dbg("qrt", QRT[:, :, :], b_QRT, [128, 8, T], BF16)
                    dbg("cu", AR1[:, :], b_AR, [128, 24 * 512], BF16)
                if stages >= 4:
                    attention(i)
                if debug and i == NT - 1:
                    dbg("att", ATT[:, :, :], b_ATT, [128, 8, T], BF16)
                if stages >= 5:
                    merge_out()
                if debug and i == NT - 1:
                    dbg("h2", XT[:, :, :], b_XT, [128, 8, T], F32)
                if stages >= 6:
                    ffn("ffn2")
                rb, rbb = rms_rb([(XT[:, c, :], [b_XT[c]], 128) for c in range(8)], D)
                for c in range(8):
                    stt(XT[:, c, :], XT[:, c, :], CST[:, C_FIN + c:C_FIN + c + 1], rb[:, :], ALU.mult, ALU.mult,
                        rd=[b_XT[c], b_CST, rbb], wr=[b_XT[c]])
                tk.op("pool", lambda e, c0=c0: e.dma_start(
                    out=outT_d[:, c0:c0 + T].rearrange("(c p) t -> p c t", p=128), in_=XT[:, :, :]),
                    rd=b_XT, wr=[b_OUT[i]], dkey="o")
            tk.op("pool", None, rd=b_OUT + list(dbg_d.values()))
            tk.op("sp", None, rd=b_OUT + list(dbg_d.values()))
            tk.finalize()
            tk.alloc_sems(semstack)
            tk.run_block()
    return nc


def _pack_consts(inp):
    c = np.zeros((128, NCOL), np.float32)

    def put(col, vec):
        v = np.asarray(vec, np.float32).reshape(-1)
        n = (v.size + 127) // 128
        for k in range(n):
            seg = v[k * 128:(k + 1) * 128]
            c[:seg.size, col + k] = seg

    put(C_FFN1, inp["ffn1_norm"][0])
    put(C_MIX, inp["mix_norm"][0])
    put(C_FFN2, inp["ffn2_norm"][0])
    put(C_FIN, inp["final_norm"][0])
    put(C_QL, inp["mla_q_lora_norm"][0])
    put(C_KVL, inp["mla_kv_lora_norm"][0])
    put(C_HGO, inp["hg_out_norm"][0])
    put(C_GQN, inp["q_head_norm"][0][:128])
    put(C_GQR, inp["q_head_norm"][0][128:])
    put(C_GKN, inp["k_head_norm"][0][:128])
    put(C_GKR, inp["k_head_norm"][0][128:])
    put(C_BM, inp["b_merge"][0])
    put(C_LB0, inp["hg_lb_table"][0])
    put(C_LB1, inp["hg_lb_table"][1])
    invf = (10000.0 ** (-np.arange(0, 64, 2, dtype=np.float32) / 64)).astype(np.float32)
    c[:64, C_INVF] = np.concatenate([invf, invf]) / np.float32(2 * math.pi)
    c[:32, C_S2PI] = TWO_PI
    c[32:64, C_S2PI] = -TWO_PI
    return c


def _const_tables():
    s = np.arange(128)[:, None]
    t = np.arange(128)[None, :]
    mask = ((s // 64 == t // 64) & (s <= t)).astype(np.float32)
    reset = np.ones((128, T), np.float32)
    reset[:, ::64] = 0.0
    ident = np.eye(128, dtype=np.float32).astype(ml_dtypes.bfloat16)
    ones = np.ones((128, 128), np.float32).astype(ml_dtypes.bfloat16)
    return mask, reset, ident, ones


def make_in_maps(inp, nb, S):
    consts = _pack_consts(inp)
    mask, reset, ident, ones = _const_tables()
    ws = {n: np.ascontiguousarray(np.asarray(inp[n])[0], dtype=np.float32) for n in WSHAPES}
    maps = []
    for b in range(nb):
        m = {"xT": np.ascontiguousarray(np.asarray(inp["x"])[b, :S].T, dtype=np.float32),
             "pos": np.ascontiguousarray(np.asarray(inp["positions"])[b:b + 1, :S], dtype=np.int32),
             "consts": consts, "mask128": mask, "resetm": reset, "ident": ident, "onesb": ones}
        m.update(ws)
        maps.append(m)
    return maps


def kernel(**inputs):
    x = np.asarray(inputs["x"])
    B, S, _ = x.shape
    nc = build(S // T)
    maps = make_in_maps(inputs, B, S)
    res = run_bass_kernel_spmd(nc, maps, core_ids=list(range(B)))
    out = np.stack([np.asarray(r["outT"]).T for r in res.results], axis=0)
    return np.ascontiguousarray(out.astype(np.float32))
```
